# Optimizing a Trainium2 kernel written in Bass

```python
import jax, jax.numpy as jnp
from jax import lax
import numpy as np


D_MODEL = 1024
BATCH = 32
SEQ = 2048
DEPTH = 4

GRID_W = 64
CTX_LEN = 256
N_MIXERS = 2
N_FOURIER_GROUPS = 4
FOURIER_GROUP = D_MODEL // N_FOURIER_GROUPS
D_RNN = (4 * D_MODEL // 3) // 64 * 64
N_RNN_BLOCKS = 16
RNN_BLOCK = D_RNN // N_RNN_BLOCKS
CONV_W = 4
CONV_LEFT = CONV_W // 2
RG_C = 8.0
D_FF = 4 * D_MODEL
N_FOURIER_LAYERS = (DEPTH + 1) // 2
N_RNN_LAYERS = DEPTH // 2
EPS = 1e-6

kernel_name = 'hybrid_fnet_rglru_prefix_dit'


def rmsnorm(x, g):
    xf = x.astype(jnp.float32)
    y = xf * lax.rsqrt(jnp.mean(xf * xf, axis=-1, keepdims=True) + EPS)
    return (y * g.astype(jnp.float32)).astype(x.dtype)


def modulate(h, shift, scale):
    return h * (1 + scale) + shift


def fourier_mix(h, w_out):
    B, T, D = h.shape
    hg = h.astype(jnp.float32).reshape(B, T, N_FOURIER_GROUPS, FOURIER_GROUP)
    f = jnp.fft.fftn(hg, axes=(1, 3), norm='ortho').real
    return f.reshape(B, T, D).astype(h.dtype) @ w_out


def centred_dwconv(u, w, b):
    T = u.shape[1]
    up = jnp.pad(u, ((0, 0), (CONV_LEFT, CONV_W - 1 - CONV_LEFT), (0, 0)))
    out = b
    for k in range(CONV_W):
        out = out + w[k] * up[:, k:k + T]
    return out


def _combine(l, r):
    a_l, b_l = l
    a_r, b_r = r
    return a_l * a_r, a_r * b_l + b_r


def linear_recurrence(a, b, h0, reverse):
    if reverse:
        a, b = jnp.flip(a, 1), jnp.flip(b, 1)
    if h0 is not None:
        b = b.at[:, 0].add(a[:, 0] * h0)
    _, h = lax.associative_scan(_combine, (a, b), axis=1)
    if reverse:
        h = jnp.flip(h, 1)
    return h


def rglru_direction(xr, w_a, b_a, w_i, b_i, lam, h0, reverse):
    B, T, _ = xr.shape
    xb = xr.reshape(B, T, N_RNN_BLOCKS, RNN_BLOCK)
    r = jax.nn.sigmoid(jnp.einsum('btni,nij->btnj', xb, w_a).reshape(B, T, D_RNN) + b_a)
    ig = jax.nn.sigmoid(jnp.einsum('btni,nij->btnj', xb, w_i).reshape(B, T, D_RNN) + b_i)
    log_a = (-RG_C * r.astype(jnp.float32)) * jax.nn.softplus(-lam.astype(jnp.float32))
    a = jnp.exp(log_a)
    mult = jnp.sqrt(-jnp.expm1(2.0 * log_a))
    bterm = mult * (ig * xr).astype(jnp.float32)
    return linear_recurrence(a, bterm, h0, reverse)


def rglru_branch(h, w_in, conv_w, conv_b, w_a, b_a, w_i, b_i, lam, h0_f, h0_b):
    u = h @ w_in
    gate, xr = jnp.split(u, 2, axis=-1)
    xr = centred_dwconv(xr, conv_w, conv_b)
    hf = rglru_direction(xr, w_a[0], b_a[0], w_i[0], b_i[0], lam[0], h0_f, False)
    hb = rglru_direction(xr, w_a[1], b_a[1], w_i[1], b_i[1], lam[1], h0_b, True)
    return hf, hb, gate


def rglru_readout(hf, hb, gate, w_out):
    return ((hf + hb).astype(gate.dtype) * jax.nn.gelu(gate)) @ w_out


def sq_relu_mlp(h, w1, b1, w2, b2):
    return jnp.square(jax.nn.relu(h @ w1 + b1)) @ w2 + b2


def setup_inputs(seed: int = 0) -> dict:
    key = jax.random.key(seed)
    ks = jax.random.split(key, 24)
    D = D_MODEL
    nr, nf = N_RNN_LAYERS, N_FOURIER_LAYERS
    f32 = jnp.float32
    nrm = lambda k, shape, s: jax.random.normal(k, shape, f32) * s
    a0 = jax.random.uniform(ks[15], (nr, 2, D_RNN), f32, minval=0.9, maxval=0.999)
    s = a0 ** (1.0 / RG_C)
    lam = jnp.log(s) - jnp.log1p(-s)
    return {
        'x': nrm(ks[0], (BATCH, SEQ, D), 1.0),
        'c': nrm(ks[1], (BATCH, D), 1.0),
        'ctx': nrm(ks[2], (BATCH, CTX_LEN, D), 1.0),
        'c_ctx': nrm(ks[3], (D,), 1.0),
        'w_mod': nrm(ks[4], (DEPTH, D, 6 * D), 0.5 * D ** -0.5),
        'b_mod': nrm(ks[5], (DEPTH, 6 * D), 0.02),
        'norm_g': 1.0 + nrm(ks[6], (DEPTH, 2, D), 0.02),
        'w_fourier': nrm(ks[7], (nf, D, D), D ** -0.5),
        'w_rnn_in': nrm(ks[8], (nr, D, 2 * D_RNN), D ** -0.5),
        'conv_w': nrm(ks[9], (nr, CONV_W, D_RNN), CONV_W ** -0.5),
        'conv_b': nrm(ks[10], (nr, D_RNN), 0.02),
        'w_a': nrm(ks[11], (nr, 2, N_RNN_BLOCKS, RNN_BLOCK, RNN_BLOCK), RNN_BLOCK ** -0.5),
        'b_a': nrm(ks[12], (nr, 2, D_RNN), 0.02),
        'w_i': nrm(ks[13], (nr, 2, N_RNN_BLOCKS, RNN_BLOCK, RNN_BLOCK), RNN_BLOCK ** -0.5),
        'b_i': nrm(ks[14], (nr, 2, D_RNN), 0.02),
        'lam': lam,
        'w_rnn_out': nrm(ks[16], (nr, D_RNN, D), D_RNN ** -0.5),
        'w1': nrm(ks[17], (DEPTH, D, D_FF), D ** -0.5),
        'b1': nrm(ks[18], (DEPTH, D_FF), 0.02),
        'w2': nrm(ks[19], (DEPTH, D_FF, D), D_FF ** -0.5),
        'b2': nrm(ks[20], (DEPTH, D), 0.02),
        'final_g': 1.0 + nrm(ks[21], (D,), 0.02),
    }


def reference(x, c, ctx, c_ctx, w_mod, b_mod, norm_g, w_fourier, w_rnn_in, conv_w, conv_b,
              w_a, b_a, w_i, b_i, lam, w_rnn_out, w1, b1, w2, b2, final_g):
    s_c = jax.nn.silu(c)
    s_cc = jax.nn.silu(c_ctx)
    for i in range(DEPTH):
        last = i == DEPTH - 1
        j = i // N_MIXERS
        mod_x = (s_c @ w_mod[i] + b_mod[i])[:, None, :]
        mod_c = (s_cc @ w_mod[i] + b_mod[i])[None, None, :]
        shx, scx, gx, shx2, scx2, gx2 = jnp.split(mod_x, 6, axis=-1)
        shc, scc, gc, shc2, scc2, gc2 = jnp.split(mod_c, 6, axis=-1)
        hx = modulate(rmsnorm(x, norm_g[i, 0]), shx, scx)
        if i % N_MIXERS == 0:
            yx = fourier_mix(hx, w_fourier[j])
            if not last:
                hc = modulate(rmsnorm(ctx, norm_g[i, 0]), shc, scc)
                yc = fourier_mix(hc, w_fourier[j])
        else:
            hc = modulate(rmsnorm(ctx, norm_g[i, 0]), shc, scc)
            p = (w_rnn_in[j], conv_w[j], conv_b[j], w_a[j], b_a[j], w_i[j], b_i[j], lam[j])
            hf_c, hb_c, gate_c = rglru_branch(hc, *p, None, None)
            hf_x, hb_x, gate_x = rglru_branch(hx, *p, hf_c[:, -1], hb_c[:, 0])
            yx = rglru_readout(hf_x, hb_x, gate_x, w_rnn_out[j])
            if not last:
                yc = rglru_readout(hf_c, hb_c, gate_c, w_rnn_out[j])
        x = x + gx * yx
        x = x + gx2 * sq_relu_mlp(modulate(rmsnorm(x, norm_g[i, 1]), shx2, scx2), w1[i], b1[i], w2[i], b2[i])
        if not last:
            ctx = ctx + gc * yc
            ctx = ctx + gc2 * sq_relu_mlp(modulate(rmsnorm(ctx, norm_g[i, 1]), shc2, scc2), w1[i], b1[i], w2[i], b2[i])
    return rmsnorm(x, final_g)
```

```python
import math
from bisect import bisect_left
from contextlib import ExitStack

import numpy as np
import ml_dtypes
import concourse.bass as bass
import concourse.mybir as mybir
from concourse.bass_utils import run_bass_kernel_spmd

F32 = mybir.dt.float32
BF16 = mybir.dt.bfloat16
AF = mybir.ActivationFunctionType
ALU = mybir.AluOpType

D = 1024
T = 2048
CT = 256
TOK = T + CT
DFF = 4096
DR = 1344
RB = 84
NRB = 16
DEPTH = 4
EPS = 1e-6
ENGS = ("pe", "act", "dve", "pool", "sp")
TILES = [(0, 256, True)] + [(256 + 512 * i, 512, False) for i in range(4)]
AR_BYTES = 92672


def _ovl(a, b):
    for (l0, h0), (l1, h1) in zip(a, b):
        if h0 <= l1 or h1 <= l0:
            return False
    return True


def _inside(a, b):
    for (l0, h0), (l1, h1) in zip(a, b):
        if l0 < l1 or h0 > h1:
            return False
    return True


class Rec:
    __slots__ = ("box", "op", "w")

    def __init__(self, box, op, w):
        self.box, self.op, self.w = box, op, w


class Op:
    __slots__ = ("idx", "eng", "fn", "dma", "deps", "signal", "seq", "waits", "epoch")

    def __init__(self, idx, eng, fn, dma, epoch):
        self.idx, self.eng, self.fn, self.dma, self.epoch = idx, eng, fn, dma, epoch
        self.deps = ()
        self.signal = False
        self.seq = 0
        self.waits = ()


class View:
    __slots__ = ("buf", "ap", "box")

    def __init__(self, buf, ap, box):
        self.buf, self.ap, self.box = buf, ap, box

    def re(self, pat, **kw):
        return View(self.buf, self.ap.rearrange(pat, **kw), self.box)

    def bc_mid(self, n):
        p, f = self.ap.shape
        return View(self.buf, self.ap.unsqueeze(1).to_broadcast([p, n, f]), self.box)

    def bc_last(self, n):
        p, c = self.ap.shape
        return View(self.buf, self.ap.unsqueeze(2).to_broadcast([p, c, n]), self.box)


class Buf:
    def __init__(self, name, ap, shape, coarse=False, inherit=(), tracked=True):
        self.name, self.ap, self.shape = name, ap, tuple(shape)
        self.coarse = coarse
        self.recs = []
        self.inherit = list(inherit)
        self.tracked = tracked
        self.full = tuple((0, s) for s in self.shape)

    def __getitem__(self, idx):
        if not isinstance(idx, tuple):
            idx = (idx,)
        ap = self.ap[idx]
        if self.coarse:
            return View(self, ap, self.full)
        box = []
        for d in range(len(self.shape)):
            if d < len(idx):
                i = idx[d]
                if isinstance(i, int):
                    box.append((i, i + 1))
                else:
                    st, sp, step = i.indices(self.shape[d])
                    if step > 0:
                        box.append((st, sp))
                    else:
                        box.append((sp + 1, st + 1))
            else:
                box.append((0, self.shape[d]))
        return View(self, ap, tuple(box))


class Bank:
    def __init__(self, psa, i):
        self.psa, self.off = psa, 512 * i

    def __getitem__(self, idx):
        p, c = idx
        st, sp, _ = c.indices(512)
        return self.psa[p, self.off + st:self.off + sp]


class Prog:
    def __init__(self):
        self.ops = []
        self.epoch = 0

    def add(self, eng, fn, w=(), r=(), dma=None, extra=()):
        op = Op(len(self.ops), eng, fn, dma, self.epoch)
        deps = set(extra)
        for v in r:
            b = v.buf
            if b is None or not b.tracked:
                continue
            deps.update(b.inherit)
            for rec in b.recs:
                if rec.w and _ovl(rec.box, v.box):
                    deps.add(rec.op)
        for v in w:
            b = v.buf
            if b is None or not b.tracked:
                continue
            deps.update(b.inherit)
            for rec in b.recs:
                if _ovl(rec.box, v.box):
                    deps.add(rec.op)
        for v in r:
            b = v.buf
            if b is None or not b.tracked:
                continue
            if dma is None:
                b.recs = [x for x in b.recs if x.w or x.op.eng != eng or x.op.dma is not None
                          or not _inside(x.box, v.box)]
            b.recs.append(Rec(v.box, op, False))
        for v in w:
            b = v.buf
            if b is None or not b.tracked:
                continue
            b.recs = [x for x in b.recs if not _inside(x.box, v.box)]
            b.recs.append(Rec(v.box, op, True))
        deps.discard(op)
        op.deps = deps
        self.ops.append(op)
        return op

    @staticmethod
    def _skip(d, op):
        return d.dma is None and op.dma is None and d.eng == "pe" and op.eng == "pe"

    def finalize(self):
        for op in self.ops:
            for d in op.deps:
                if d.dma is None and not self._skip(d, op):
                    d.signal = True
        cnt = {}
        for op in self.ops:
            if op.dma is None and op.signal:
                k = (op.eng, op.epoch)
                cnt[k] = cnt.get(k, 0) + 1
                op.seq = cnt[k]
        self.keyops = {}
        for op in self.ops:
            if op.dma is not None:
                self.keyops.setdefault(op.dma, []).append(op.idx)
        known = {e: {} for e in ENGS}
        for op in self.ops:
            need = {}
            for d in op.deps:
                if d.dma is not None:
                    n = bisect_left(self.keyops[d.dma], d.idx) + 1
                    sem, val = ("dma", d.dma), 16 * n
                else:
                    if self._skip(d, op):
                        continue
                    sem, val = ("eng", d.eng, d.epoch), d.seq
                if need.get(sem, 0) < val:
                    need[sem] = val
            k = known[op.eng]
            waits = []
            for sem, val in need.items():
                if k.get(sem, 0) < val:
                    k[sem] = val
                    waits.append((sem, val))
            op.waits = waits
        self.engsems = sorted(cnt.keys())
        self.maxcnt = max(cnt.values()) if cnt else 0

    def emit(self, nc, st):
        sems = {}
        for (e, ep) in self.engsems:
            sems[("eng", e, ep)] = st.enter_context(nc.semaphore(f"s_{e}_{ep}"))
        for k in self.keyops:
            sems[("dma", k)] = st.enter_context(nc.semaphore(f"d_{k}"))
        block = st.enter_context(nc.Block())
        per = {e: [op for op in self.ops if op.eng == e] for e in ENGS}

        def run(name):
            def f(eng):
                for op in per[name]:
                    for sem, val in op.waits:
                        eng.wait_ge(sems[sem], val)
                    if op.fn is not None:
                        ins = op.fn(eng)
                        if op.dma is not None:
                            ins.then_inc(sems[("dma", op.dma)], 16)
                        elif op.signal:
                            ins.then_inc(sems[("eng", name, op.epoch)], 1)
            return f

        block.sync(run("sp"))
        block.tensor(run("pe"))
        block.scalar(run("act"))
        block.vector(run("dve"))
        block.gpsimd(run("pool"))


SV_LAYOUT = {}
_o = 0
for _n, _c in (("g", 64), ("fg", 8), ("bmod", 192), ("b1", 128), ("b2", 32), ("cw", 128), ("cb", 32),
               ("ba", 64), ("bi", 64), ("lam", 64)):
    SV_LAYOUT[_n] = (_o, _c)
    _o += _c
NSV = _o


def _host_sv(norm_g, final_g, b_mod, b1, b2, conv_w, conv_b, b_a, b_i, lam):
    sv = np.zeros((128, NSV), np.float32)

    def put(name, arr):
        o, c = SV_LAYOUT[name]
        a = np.ascontiguousarray(arr, dtype=np.float32).reshape(arr.shape[0], -1)
        assert a.shape[1] == c, (name, a.shape)
        sv[: a.shape[0], o:o + c] = a

    put("g", norm_g.reshape(4, 2, 8, 128).transpose(3, 0, 1, 2))
    put("fg", final_g.reshape(8, 128).transpose(1, 0))
    put("bmod", b_mod.reshape(4, 48, 128).transpose(2, 0, 1))
    put("b1", b1.reshape(4, 32, 128).transpose(2, 0, 1))
    put("b2", b2.reshape(4, 8, 128).transpose(2, 0, 1))
    put("cw", conv_w.reshape(2, 4, 16, 84).transpose(3, 0, 1, 2))
    put("cb", conv_b.reshape(2, 16, 84).transpose(2, 0, 1))
    put("ba", b_a.reshape(2, 2, 16, 84).transpose(3, 0, 1, 2))
    put("bi", b_i.reshape(2, 2, 16, 84).transpose(3, 0, 1, 2))
    put("lam", lam.reshape(2, 2, 16, 84).transpose(3, 0, 1, 2))
    return sv


def _host_consts():
    bf = ml_dtypes.bfloat16
    ident = np.eye(128, dtype=np.float32)
    ones = np.ones((128, 128), dtype=bf)
    c = np.arange(256, dtype=np.float64)[:, None]
    cp = np.arange(256, dtype=np.float64)[None, :]
    ang = 2 * np.pi * ((c * cp) % 256) / 256
    cs = np.concatenate([np.cos(ang), np.sin(ang)], axis=1) / 16.0
    csc = cs.reshape(2, 128, 512).transpose(1, 0, 2).astype(bf)
    t = np.arange(T, dtype=np.int64)[:, None]
    k = np.arange(T, dtype=np.int64)[None, :]
    ang = 2 * np.pi * ((t * k) % T).astype(np.float64) / T
    s = 1.0 / math.sqrt(T)
    cosm = (np.cos(ang) * s)[:, :T // 2].reshape(16, 128, 4, 256)
    sinm = (np.sin(ang) * s)[:, :T // 2].reshape(16, 128, 4, 256)
    dftx = np.stack([cosm, sinm], axis=0).transpose(3, 2, 0, 1, 4)
    dftx = np.ascontiguousarray(dftx).astype(bf)
    alt = np.zeros((128, 2), dtype=np.float64)
    alt[:, 0] = np.where(np.arange(128) % 2 == 0, 1.0, -1.0) / math.sqrt(T)
    alt = alt.astype(bf)
    t = np.arange(CT, dtype=np.int64)[:, None]
    k = np.arange(CT, dtype=np.int64)[None, :]
    ang = 2 * np.pi * ((t * k) % CT).astype(np.float64) / CT
    s = 1.0 / math.sqrt(CT)
    cosm = (np.cos(ang) * s).reshape(2, 128, 256)
    sinm = (-np.sin(ang) * s).reshape(2, 128, 256)
    dftc = np.ascontiguousarray(np.stack([cosm, sinm], axis=0).transpose(2, 0, 1, 3)).astype(bf)
    onesrow = np.ones((1, 2320), dtype=bf)
    return dict(ident=ident, ones=ones, csc=csc, dftx=dftx, dftc=dftc, onesrow=onesrow, alt=alt)


class K:
    def __init__(self, NB, layers=(0, 1, 2, 3), do_mixer=True, do_mlp=True):
        self.NB = NB
        self.layers = tuple(layers)
        self.do_mixer, self.do_mlp = do_mixer, do_mlp
        self.P = Prog()
        self.nc = bass.Bass("TRN2", target_bir_lowering=False)

    def mm(self, out, lhsT, rhs, start, stop):
        self.P.add("pe", lambda e: e.matmul(out.ap, lhsT.ap, rhs.ap, start=start, stop=stop),
                   w=[out], r=[lhsT, rhs])

    def transpose(self, out, in_, ident):
        self.P.add("pe", lambda e: e.transpose(out.ap, in_.ap, ident.ap), w=[out], r=[in_, ident])

    def act(self, out, in_, func, bias=None, scale=None):
        r = [in_]
        kw = {}
        if bias is not None:
            if isinstance(bias, View):
                r.append(bias)
                kw["bias"] = bias.ap
            else:
                kw["bias"] = float(bias)
        if scale is not None:
            if isinstance(scale, View):
                r.append(scale)
                kw["scale"] = scale.ap
            else:
                kw["scale"] = float(scale)
        self.P.add("act", lambda e: e.activation(out.ap, in_.ap, func, **kw), w=[out], r=r)

    def tt(self, out, in0, in1, op, eng="dve"):
        self.P.add(eng, lambda e: e.tensor_tensor(out.ap, in0.ap, in1.ap, op), w=[out], r=[in0, in1])

    def stt(self, out, in0, scalar, in1, op0, op1, eng="dve"):
        r = [in0, in1]
        if isinstance(scalar, View):
            r.append(scalar)
            sc = scalar.ap
        else:
            sc = float(scalar)
        self.P.add(eng, lambda e: e.scalar_tensor_tensor(out.ap, in0.ap, sc, in1.ap, op0, op1), w=[out], r=r)

    def ts(self, out, in0, s1, s2, op0, op1=None, eng="dve"):
        r = [in0]
        a1 = s1.ap if isinstance(s1, View) else float(s1)
        if isinstance(s1, View):
            r.append(s1)
        if s2 is None:
            self.P.add(eng, lambda e: e.tensor_scalar(out.ap, in0.ap, a1, None, op0), w=[out], r=r)
            return
        a2 = s2.ap if isinstance(s2, View) else float(s2)
        if isinstance(s2, View):
            r.append(s2)
        self.P.add(eng, lambda e: e.tensor_scalar(out.ap, in0.ap, a1, a2, op0, op1), w=[out], r=r)

    def copy(self, out, in_, eng="dve"):
        if eng == "act":
            self.act(out, in_, AF.Copy)
        else:
            self.P.add(eng, lambda e: e.tensor_copy(out.ap, in_.ap), w=[out], r=[in_])

    def scan(self, out, d0, d1, initial):
        r = [d0, d1]
        if isinstance(initial, View):
            r.append(initial)
            ini = initial.ap
        else:
            ini = float(initial)
        self.P.add("dve", lambda e: e.tensor_tensor_scan(out.ap, d0.ap, d1.ap, ini, ALU.mult, ALU.add),
                   w=[out], r=r)

    def dma(self, q, out, in_, key, w=None, r=None):
        ws = [out] if w is None else w
        rs = [in_] if r is None else r
        return self.P.add(q, lambda e: e.dma_start(out=out.ap, in_=in_.ap), w=ws, r=rs, dma=key)

    def abuf(self, name, off, shape, dt, coarse=False):
        esz = 4 if dt == F32 else 2
        n = 1
        for s in shape[1:]:
            n *= s
        nbytes = n * esz
        assert off % 4 == 0 and off + nbytes <= AR_BYTES, (name, off, nbytes)
        ap = self.ar_t[0:shape[0], off // 2:(off + nbytes) // 2]
        if dt == F32:
            ap = ap.bitcast(F32)
        if len(shape) == 3:
            ap = ap.rearrange("p (a b) -> p a b", a=shape[1])
        elif len(shape) == 4:
            ap = ap.rearrange("p (a b c) -> p a b c", a=shape[1], b=shape[2])
        inh = {}
        live = []
        for (o2, n2, b2) in self.ar_live:
            if o2 < off + nbytes and off < o2 + n2:
                for rec in b2.recs:
                    kk = ("dma", rec.op.dma) if rec.op.dma is not None else ("eng", rec.op.eng)
                    if kk not in inh or inh[kk].idx < rec.op.idx:
                        inh[kk] = rec.op
                for d in b2.inherit:
                    kk = ("dma", d.dma) if d.dma is not None else ("eng", d.eng)
                    if kk not in inh or inh[kk].idx < d.idx:
                        inh[kk] = d
                if not (off <= o2 and o2 + n2 <= off + nbytes):
                    live.append((o2, n2, b2))
            else:
                live.append((o2, n2, b2))
        b = Buf(name, ap, shape, coarse=coarse, inherit=list(inh.values()))
        live.append((off, nbytes, b))
        self.ar_live = live
        return b

    def build(self):
        nc, NB = self.nc, self.NB
        st = ExitStack()
        self.st = st
        dr = {}

        def din(name, shape, dt=F32):
            dr[name] = Buf(name, nc.dram_tensor(name, list(shape), dt, kind="ExternalInput").ap(), shape,
                           tracked=False)
            return dr[name]

        din("x", (NB, T, D))
        din("ctx", (NB, CT, D))
        din("cT", (128, 8, NB + 1))
        din("sv", (128, NSV))
        din("w_mod", (4, D, 6 * D))
        din("w_fourier", (2, D, D))
        din("w_rnn_in", (2, D, 2 * DR))
        din("w_a", (2, 2, NRB, RB, RB))
        din("w_i", (2, 2, NRB, RB, RB))
        din("w_rnn_out", (2, DR, D))
        din("w1", (4, D, DFF))
        din("w2", (4, DFF, D))
        din("ident", (128, 128))
        din("ones", (128, 128), BF16)
        din("csc", (128, 2, 512), BF16)
        din("dftx", (4, 128, 2, 16, 256), BF16)
        din("alt", (128, 2), BF16)
        din("dftc", (128, 2, 2, 256), BF16)
        din("conv_b", (2, DR))
        din("onesrow", (1, 2320), BF16)
        self.dr = dr
        self.out = Buf("out", nc.dram_tensor("out", [NB, T, D], F32, kind="ExternalOutput").ap(), (NB, T, D),
                       tracked=False)

        def sb(name, shape, dt):
            return st.enter_context(nc.sbuf_tensor(name, list(shape), dt))

        self.X = Buf("X", sb("X", (128, 8, TOK), F32)[:], (128, 8, TOK))
        self.H = Buf("H", sb("H", (128, 8, TOK), BF16)[:], (128, 8, TOK))
        self.ar_t = sb("AR", (128, AR_BYTES // 2), BF16)
        self.ar_live = []
        self.SV = Buf("SV", sb("SV", (128, NSV), F32)[:], (128, NSV), coarse=True)
        self.MODS = Buf("MODS", sb("MODS", (128, 4, 48, NB + 1), F32)[:], (128, 4, 48, NB + 1))
        self.SCAL = [Buf(f"SCAL{i}", sb(f"SCAL{i}", (128, 2, 3, 8), F32)[:], (128, 2, 3, 8)) for i in range(2)]
        self.RC = Buf("RC", sb("RC", (RB, 4, 64), F32)[:], (RB, 4, 64), coarse=True)
        self.IDENT = Buf("IDENT", sb("IDENT", (128, 128), F32)[:], (128, 128), coarse=True)
        self.ONES = Buf("ONES", sb("ONES", (128, 128), BF16)[:], (128, 128), coarse=True)
        self.CTs = Buf("CTs", sb("CTs", (128, 8, NB + 1), F32)[:], (128, 8, NB + 1), coarse=True)
        self.SCb = Buf("SCb", sb("SCb", (128, 8, NB + 1), BF16)[:], (128, 8, NB + 1), coarse=True)
        pt = st.enter_context(nc.psum_tensor("PSA", [128, 4096], F32))
        self.PSA = Buf("PSA", pt[:], (128, 4096))
        self.PS = [Bank(self.PSA, i) for i in range(8)]
        self.rot = 0

        self.stage_setup()
        cnt = 0
        for b in range(NB):
            self.P.epoch = b
            self.load_x(b)
            for l in self.layers:
                par = cnt % 2
                cnt += 1
                self.compute_scal(b, l, par)
                if self.do_mixer:
                    if l % 2 == 0:
                        self.fourier(b, l, par)
                    else:
                        self.rnn(b, l, par)
                if self.do_mlp:
                    self.mlp(b, l, par)
            self.final(b)
        last = {}
        for op in self.P.ops:
            if op.dma is not None and op.dma.startswith("so"):
                last[op.dma] = op
        self.P.add("sp", None, extra=list(last.values()))
        self.P.finalize()
        self.P.emit(nc, st)
        st.close()
        return nc

    def svv(self, name, pat=None, np_=128, **kw):
        o, c = SV_LAYOUT[name]
        ap = self.SV.ap[0:np_, o:o + c]
        if pat is not None:
            ap = ap.rearrange(pat, **kw)
        return ap

    def stage_setup(self):
        NB, dr = self.NB, self.dr
        d_sv = self.dma("sp", self.SV[:, :], dr["sv"][:, :], "sv")
        self.dma("sp", self.IDENT[:, :], dr["ident"][:, :], "ident")
        self.dma("sp", self.ONES[:, :], dr["ones"][:, :], "ones")
        self.dma("sp", self.CTs[:, :, :], dr["cT"][:, :, :], "cT")
        inh = [d_sv]
        self.G = Buf("G", self.svv("g", "p (l n k) -> p l n k", l=4, n=2), (128, 4, 2, 8), inherit=inh)
        self.FG = Buf("FG", self.svv("fg"), (128, 8), inherit=inh)
        self.BMOD = Buf("BMOD", self.svv("bmod", "p (l j) -> p l j", l=4), (128, 4, 48), inherit=inh)
        self.B1 = Buf("B1", self.svv("b1", "p (l f) -> p l f", l=4), (128, 4, 32), inherit=inh)
        self.B2 = Buf("B2", self.svv("b2", "p (l k) -> p l k", l=4), (128, 4, 8), inherit=inh)
        self.CW = Buf("CW", self.svv("cw", "p (j k n) -> p j k n", np_=RB, j=2, k=4), (RB, 2, 4, NRB), inherit=inh)
        self.CB = Buf("CB", self.svv("cb", "p (j n) -> p j n", np_=RB, j=2), (RB, 2, NRB), inherit=inh)
        self.BA = Buf("BA", self.svv("ba", "p (j d n) -> p j d n", np_=RB, j=2, d=2), (RB, 2, 2, NRB), inherit=inh)
        self.BI = Buf("BI", self.svv("bi", "p (j d n) -> p j d n", np_=RB, j=2, d=2), (RB, 2, 2, NRB), inherit=inh)
        self.LAM = Buf("LAM", self.svv("lam", "p (j d n) -> p j d n", np_=RB, j=2, d=2), (RB, 2, 2, NRB), inherit=inh)
        RC = self.RC
        LAMF = Buf("LAMF", self.svv("lam", np_=RB), (RB, 64), inherit=inh)
        BAF = Buf("BAF", self.svv("ba", np_=RB), (RB, 64), inherit=inh)
        BIF = Buf("BIF", self.svv("bi", np_=RB), (RB, 64), inherit=inh)
        self.act(RC[:, 0, :], LAMF[:, :], AF.Exp, scale=-1.0)
        self.act(RC[:, 0, :], RC[:, 0, :], AF.Ln, bias=1.0)
        self.ts(RC[:, 1, :], RC[:, 0, :], -4.0, None, ALU.mult)
        self.ts(RC[:, 0, :], RC[:, 0, :], -8.0, None, ALU.mult)
        self.ts(RC[:, 2, :], BAF[:, :], 0.5, None, ALU.mult)
        self.ts(RC[:, 3, :], BIF[:, :], 0.5, None, ALU.mult)
        self.act(self.SCb[:, :, :], self.CTs[:, :, :], AF.Silu)
        ncol = NB + 1
        WM = [self.abuf(f"WM{i}", i * 12288, (128, 8, 768), BF16, coarse=True) for i in range(2)]
        it = 0
        for l in self.layers:
            for g in range(8):
                wm = WM[it % 2]
                self.dma("pool", wm[:, :, :],
                         View(None, dr["w_mod"].ap[l][:, g * 768:(g + 1) * 768].rearrange("(k p) n -> p k n", p=128), None),
                         f"wm{it % 2}")
                ps = self.PS[it % 2]
                for jj in range(6):
                    for k in range(8):
                        self.mm(ps[:, jj * 8:jj * 8 + ncol], wm[:, k, jj * 128:(jj + 1) * 128], self.SCb[:, k, :],
                                k == 0, k == 7)
                v48 = self.PS[it % 2][:, 0:48]
                psv = View(v48.buf, v48.ap.rearrange("p (a b) -> p a b", a=6)[:, :, 0:ncol], v48.box)
                self.tt(self.MODS[:, l, g * 6:(g + 1) * 6, :], psv,
                        self.BMOD[:, l, g * 6:(g + 1) * 6].bc_last(ncol), ALU.add)
                it += 1

    def compute_scal(self, b, l, par):
        SC = self.SCAL[par]
        for s, col in ((0, b), (1, self.NB)):
            self.stt(SC[:, s, 0, :], self.MODS[:, l, 8:16, col], 1.0, self.G[:, l, 0, :], ALU.add, ALU.mult)
            self.stt(SC[:, s, 1, :], self.MODS[:, l, 32:40, col], 1.0, self.G[:, l, 1, :], ALU.add, ALU.mult)
            self.tt(SC[:, s, 2, :], self.MODS[:, l, 40:48, col], self.B2[:, l, :], ALU.mult)

    def load_x(self, b):
        dr = self.dr
        STG = [self.abuf(f"STG{i}", i * 4096, (128, 1024), F32, coarse=True) for i in range(2)]
        for tc in range(18):
            stg = STG[tc % 2]
            if tc < 2:
                src = dr["ctx"][b, tc * 128:(tc + 1) * 128, :]
            else:
                src = dr["x"][b, (tc - 2) * 128:(tc - 1) * 128, :]
            self.dma("sp", stg[:, :], src, f"stg{tc % 2}")
            pa, pb = (self.PS[0], self.PS[1]) if tc % 2 == 0 else (self.PS[2], self.PS[3])
            for j in range(8):
                pbk = pa if j < 4 else pb
                self.transpose(pbk[:, (j % 4) * 128:(j % 4 + 1) * 128], stg[:, j * 128:(j + 1) * 128], self.IDENT[:, :])
            self.act(self.X[:, 0:4, tc * 128:(tc + 1) * 128], pa[:, :].re("p (a b) -> p a b", a=4), AF.Copy)
            self.copy(self.X[:, 4:8, tc * 128:(tc + 1) * 128], pb[:, :].re("p (a b) -> p a b", a=4))

    def norm_bufs(self):
        TMP = self.abuf("NTMP", 0, (128, 8, 512), F32)
        SQ = [self.abuf(f"NSQ{i}", 16384 + i * 8192, (128, 8, 512), BF16) for i in range(2)]
        R = [self.abuf(f"NR{i}", 32768 + i * 2048, (128, 512), F32) for i in range(2)]
        return TMP, SQ, R

    def rstd(self, t0, n, SQ, R, part=3):
        ps = self.PS[7]
        if part & 1:
            self.act(SQ[:, :, 0:n], self.X[:, :, t0:t0 + n], AF.Square)
            for c in range(8):
                self.mm(ps[:, 0:n], self.ONES[:, :], SQ[:, c, 0:n], c == 0, c == 7)
        if part & 2:
            self.act(R[:, 0:n], ps[:, 0:n], AF.Ln, scale=1.0 / D, bias=EPS)
            self.act(R[:, 0:n], R[:, 0:n], AF.Exp, scale=-0.5)

    def norm_mod(self, b, l, which, tiles, par, bufs=None):
        TMP, SQ, R = bufs if bufs is not None else self.norm_bufs()
        SC = self.SCAL[par]

        def stage_a(i, part=3):
            t0, n, isctx = tiles[i]
            self.rstd(t0, n, SQ[i % 2], R[i % 2], part)

        def stage_b(i):
            t0, n, isctx = tiles[i]
            self.tt(TMP[:, :, 0:n], self.X[:, :, t0:t0 + n], R[i % 2][:, 0:n].bc_mid(8), ALU.mult)

        def stage_c(i):
            t0, n, isctx = tiles[i]
            s, col = (1, self.NB) if isctx else (0, b)
            for c in range(8):
                A = SC[:, s, which, c:c + 1]
                B = self.MODS[:, l, (24 if which else 0) + c, col:col + 1]
                if c < 4:
                    self.act(self.H[:, c, t0:t0 + n], TMP[:, c, 0:n], AF.Identity, bias=B, scale=A)
                else:
                    self.ts(self.H[:, c, t0:t0 + n], TMP[:, c, 0:n], A, B, ALU.mult, ALU.add)

        stage_a(0)
        for i in range(len(tiles)):
            stage_b(i)
            dbl = SQ[0] is not SQ[1]
            if i + 1 < len(tiles) and dbl:
                stage_a(i + 1, 1)
            stage_c(i)
            if i + 1 < len(tiles):
                stage_a(i + 1, 2 if dbl else 3)

    def mlp(self, b, l, par):
        dr = self.dr
        tiles = TILES if l < DEPTH - 1 else TILES[1:]
        W1S = [self.abuf(f"W1S{i}", 53248 + i * 8192, (128, 8, 512), BF16, coarse=True) for i in range(2)]
        W2S = [self.abuf(f"W2S{i}", 69632 + i * 8192, (128, 4, 1024), BF16, coarse=True) for i in range(2)]

        def load(g):
            self.dma("pool", W1S[g % 2][:, :, :],
                     View(None, dr["w1"].ap[l][:, g * 512:(g + 1) * 512].rearrange("(k p) n -> p k n", p=128), None),
                     f"w1s{g % 2}")
            self.dma("pool", W2S[g % 2][:, :, :],
                     View(None, dr["w2"].ap[l][g * 512:(g + 1) * 512, :].rearrange("(j p) n -> p j n", p=128), None),
                     f"w2s{g % 2}")

        load(0)
        load(1)
        nb_tmp = self.abuf("NTMP", 0, (128, 8, 512), F32)
        nb_sq = self.abuf("NSQ", 16384, (128, 8, 512), BF16)
        nb_r = self.abuf("NR", 24576, (128, 512), F32)
        nbufs = (nb_tmp, [nb_sq, nb_sq], [nb_r, nb_r])
        SC = self.SCAL[par]

        def prep_tile(ti):
            t0, n, isctx = tiles[ti]
            self.norm_mod(b, l, 1, [tiles[ti]], par, bufs=nbufs)
            s = 1 if isctx else 0
            self.tt(self.X[:, :, t0:t0 + n], self.X[:, :, t0:t0 + n], SC[:, s, 2, :].bc_last(n), ALU.add)

        RL = [self.abuf(f"RL{i}", 26624 + i * 8192, (128, 4, 512), F32) for i in range(2)]
        Z = [self.abuf(f"Z{i}", 43008 + i * 4096, (128, 4, 512), BF16) for i in range(2)]
        steps = [(g, ti) for g in range(8) for ti in range(len(tiles))]

        def h1(i):
            g, ti = steps[i]
            t0, n, isctx = tiles[ti]
            s = i % 2
            for j in range(4):
                ps = self.PS[j]
                for k in range(8):
                    self.mm(ps[:, 0:n], W1S[g % 2][:, k, j * 128:(j + 1) * 128], self.H[:, k, t0:t0 + n], k == 0, k == 7)
                self.act(RL[s][:, j, 0:n], ps[:, 0:n], AF.Relu, bias=self.B1[:, l, g * 4 + j:g * 4 + j + 1])
            self.act(Z[s][:, :, 0:n], RL[s][:, :, 0:n], AF.Square)

        def y(i):
            g, ti = steps[i]
            t0, n, isctx = tiles[ti]
            s = i % 2
            col = self.NB if isctx else b
            for dc in range(8):
                ps = self.PS[4 + dc % 2]
                for j in range(4):
                    self.mm(ps[:, 0:n], W2S[g % 2][:, j, dc * 128:(dc + 1) * 128], Z[s][:, j, 0:n], j == 0, j == 3)
                self.stt(self.X[:, dc, t0:t0 + n], ps[:, 0:n], self.MODS[:, l, 40 + dc, col:col + 1],
                         self.X[:, dc, t0:t0 + n], ALU.mult, ALU.add)

        prep_tile(0)
        h1(0)
        for i in range(len(steps)):
            if i + 1 < len(steps):
                if i + 1 < len(tiles):
                    prep_tile(i + 1)
                h1(i + 1)
            y(i)
            g, ti = steps[i]
            if ti == 0 and 1 <= g < 7:
                load(g + 1)

    def fourier(self, b, l, par):
        dr = self.dr
        j = l // 2
        CSC = self.abuf("CSC", 69632, (128, 2, 512), BF16, coarse=True)
        DFTC = self.abuf("DFTC", 71680, (128, 2, 2, 256), BF16, coarse=True)
        self.dma("sp", CSC[:, :, :], dr["csc"][:, :, :], "csc")
        self.dma("sp", DFTC[:, :, :, :], dr["dftc"][:, :, :, :], "dftc")
        ALT = self.abuf("ALT", 73728, (128, 2), BF16, coarse=True)
        self.dma("sp", ALT[:, :], dr["alt"][:, :], "alt")
        QT = [self.abuf(f"QT{i}", 73732 + i * 1024, (128, 256), F32) for i in range(2)]
        self.norm_mod(b, l, 0, TILES, par)
        UV = self.abuf("UV", 0, (128, 18, 2, 512), BF16)
        TAB = [self.abuf(f"TAB{i}", 36864 + i * 16384, (128, 2, 16, 256), BF16, coarse=True) for i in range(2)]
        ev = 0
        nt = 0
        for half in range(2):
            for tc in range(18):
                for gl in range(2):
                    g = 2 * half + gl
                    ps = self.PS[(tc * 2 + gl) % 2]
                    self.mm(ps[:, 0:512], self.H[:, 2 * g, tc * 128:(tc + 1) * 128], CSC[:, 0, :], True, False)
                    self.mm(ps[:, 0:512], self.H[:, 2 * g + 1, tc * 128:(tc + 1) * 128], CSC[:, 1, :], False, True)
                    self.copy(UV[:, tc, gl, :], ps[:, 0:512], eng=("act" if ev % 2 == 0 else "dve"))
                    ev += 1
            for kt in range(4):
                tab = TAB[nt % 2]
                self.dma("sp", tab[:, :, :, :], dr["dftx"][kt], f"tab{nt % 2}")
                nt += 1
                k0 = kt * 256
                for ccl in range(4):
                    cc = 4 * half + ccl
                    gl, co = ccl // 2, (ccl % 2) * 128
                    ps = self.PS[2 + ev % 2]
                    qt = QT[ev % 2]
                    for tcx in range(16):
                        self.mm(ps[:, 0:256], UV[:, 2 + tcx, gl, co:co + 128], tab[:, 0, tcx, :], tcx == 0, tcx == 15)
                    for tcx in range(16):
                        self.mm(ps[:, 256:512], UV[:, 2 + tcx, gl, 256 + co:256 + co + 128], tab[:, 1, tcx, :], tcx == 0, tcx == 15)
                    self.act(qt[:, :], ps[:, 256:512], AF.Copy)
                    self.tt(self.H[:, cc, CT + k0:CT + k0 + 256], ps[:, 0:256], qt[:, :], ALU.subtract)
                    if kt == 0:
                        self.tt(self.H[:, cc, CT + T - 1:CT + T - 256:-1], ps[:, 1:256], qt[:, 1:256], ALU.add)
                    else:
                        self.tt(self.H[:, cc, CT + T - k0:CT + T - k0 - 256:-1], ps[:, 0:256], qt[:, :], ALU.add)
                    ev += 1
            for ccl in range(4):
                cc = 4 * half + ccl
                gl, co = ccl // 2, (ccl % 2) * 128
                ps = self.PS[2 + ev % 2]
                for tcx in range(16):
                    self.mm(ps[:, 0:1], UV[:, 2 + tcx, gl, co:co + 128], ALT[:, 0:1], tcx == 0, tcx == 15)
                self.act(self.H[:, cc, CT + T // 2:CT + T // 2 + 1], ps[:, 0:1], AF.Copy)
                ev += 1
            for ccl in range(4):
                cc = 4 * half + ccl
                gl, co = ccl // 2, (ccl % 2) * 128
                ps = self.PS[2 + ev % 2]
                for tcx in range(2):
                    self.mm(ps[:, 0:256], UV[:, tcx, gl, co:co + 128], DFTC[:, 0, tcx, :], tcx == 0, False)
                    self.mm(ps[:, 0:256], UV[:, tcx, gl, 256 + co:256 + co + 128], DFTC[:, 1, tcx, :], False, tcx == 1)
                self.copy(self.H[:, cc, 0:256], ps[:, 0:256], eng=("act" if ev % 2 == 0 else "dve"))
                ev += 1
        WF = self.abuf("WF", 36864, (128, 8, 1024), BF16, coarse=True)
        self.dma("pool", WF[:, :, :],
                 View(None, dr["w_fourier"].ap[j].rearrange("(k p) n -> p k n", p=128), None), "wf")
        for (t0, n, isctx) in TILES:
            col = self.NB if isctx else b
            for dc in range(8):
                ps = self.PS[4 + dc % 2]
                for k in range(8):
                    self.mm(ps[:, 0:n], WF[:, k, dc * 128:(dc + 1) * 128], self.H[:, k, t0:t0 + n], k == 0, k == 7)
                self.stt(self.X[:, dc, t0:t0 + n], ps[:, 0:n], self.MODS[:, l, 16 + dc, col:col + 1],
                         self.X[:, dc, t0:t0 + n], ALU.mult, ALU.add)

    def abank(self):
        self.rot = (self.rot + 1) % 2
        return self.PS[4 + self.rot]

    def robank(self):
        self.rot2 = (getattr(self, "rot2", 0) + 1) % 2
        return self.PS[6 + self.rot2]

    def rnn(self, b, l, par):
        dr = self.dr
        j = l // 2
        last = (l == DEPTH - 1)
        o = 64576
        XB = self.abuf("XB", o, (RB, TOK), BF16); o += 4608
        WINX, WING = [], []
        for i in range(2):
            WINX.append(self.abuf(f"WINX{i}", o, (128, 8, RB), BF16, coarse=True)); o += 1344
            WING.append(self.abuf(f"WING{i}", o, (128, 8, RB), BF16, coarse=True)); o += 1344
        WG = []
        for i in range(2):
            WG.append(self.abuf(f"WG{i}", o, (RB, 4, RB), BF16, coarse=True)); o += 672
        YR2 = self.abuf("YR2", o, (RB, 2, TOK), BF16); o += 9216
        WO = self.abuf("WO", o, (RB, 2, 1024), BF16, coarse=True); o += 4096
        o_tail = o
        assert o + 2368 <= AR_BYTES, o

        wsrc = dr["w_rnn_in"].ap[j]

        def load_g(n):
            self.dma("pool", WING[n % 2][:, :, :],
                     View(None, wsrc[:, RB * n:RB * (n + 1)].rearrange("(k p) n -> p k n", p=128), None), f"wing{n % 2}")

        def load_x(n):
            self.dma("pool", WINX[n % 2][:, :, :],
                     View(None, wsrc[:, DR + RB * n:DR + RB * (n + 1)].rearrange("(k p) n -> p k n", p=128), None),
                     f"winx{n % 2}")

        def load_wg(n):
            wg = WG[n % 2]
            for d in range(2):
                self.dma("pool", View(wg, wg.ap[:, 2 * d, :], wg.full), View(None, dr["w_a"].ap[j, d, n], None), f"wg{n % 2}")
                self.dma("pool", View(wg, wg.ap[:, 2 * d + 1, :], wg.full), View(None, dr["w_i"].ap[j, d, n], None), f"wg{n % 2}")

        def loadwo(n):
            self.dma("pool", WO[:, :, :],
                     View(None, dr["w_rnn_out"].ap[j][RB * n:RB * (n + 2), :].rearrange("(q p) d -> p q d", p=RB), None),
                     "wo")

        for i in range(2):
            load_x(i)
            load_wg(i)
            load_g(i)
        loadwo(0)
        self.norm_mod(b, l, 0, TILES, par)
        XRb = self.abuf("XRb", 0, (RB + 1, 2320), BF16)
        YR3 = self.abuf("YR3", 4640, (RB, 1, TOK), BF16)
        DG = [self.abuf(f"DG{i}", o_tail + i * 672, (RB + 1, 4, RB), BF16) for i in range(2)]
        XCc = self.abuf("XCc", o_tail + 1344, (RB, CT), F32)
        AA = [self.abuf(f"A{d}", 9280 + 27648 * d, (RB, TOK), F32) for d in range(2)]
        MM = [self.abuf(f"M{d}", 18496 + 27648 * d, (RB, TOK), F32) for d in range(2)]
        TT = [self.abuf(f"T{d}", 27712 + 27648 * d, (RB, TOK), F32) for d in range(2)]

        def yr(slot, lo, hi):
            return YR2[:, slot, lo:hi] if slot < 2 else YR3[:, 0, lo:hi]
        RC = self.RC
        PSA = self.PSA
        v = XRb[:, :]
        self.P.add("dve", lambda e: e.memset(v.ap, 0.0), w=[v])
        self.dma("sp", XRb[RB:RB + 1, :], dr["onesrow"][:, :], "onesrow")
        pcol = lambda t0: (t0 - CT) if t0 >= CT else T
        xcol = lambda t0: (261 + t0 - CT) if t0 >= CT else (2 + t0)

        def prep_dg(n):
            dg = DG[n % 2]
            for kk in range(4):
                self.ts(dg[0:RB, kk, :], self.IDENT[0:RB, 0:RB], self.CW[:, j, kk, n:n + 1], None, ALU.mult)
            self.dma("pool", dg[RB:RB + 1, 2, :], dr["conv_b"][j:j + 1, RB * n:RB * (n + 1)], f"dgb{n % 2}")

        def xr_tile(n, ti):
            t0, nn, isctx = TILES[ti]
            ps = self.abank()
            for k in range(8):
                self.mm(ps[0:RB, 0:nn], WINX[n % 2][:, k, :], self.H[:, k, t0:t0 + nn], k == 0, k == 7)
            c0 = xcol(t0)
            self.act(XRb[0:RB, c0:c0 + nn], ps[0:RB, 0:nn], AF.Copy)

        cps = [None]

        def conv_mm(n):
            dg = DG[n % 2]
            cps[0] = self.abank()
            for (t0, nn, isctx) in TILES:
                c0 = xcol(t0)
                dst = cps[0][0:RB, 0:nn] if isctx else PSA[0:RB, t0 - CT:t0 - CT + nn]
                for kk in range(4):
                    kp = RB + 1 if kk == 2 else RB
                    self.mm(dst, dg[0:kp, kk, :], XRb[0:kp, c0 + kk - 2:c0 + kk - 2 + nn], kk == 0, kk == 3)

        def conv_ev():
            self.act(XCc[:, :], cps[0][0:RB, 0:CT], AF.Copy)

        def cast_xb():
            self.act(XB[:, 0:CT], XCc[:, :], AF.Copy)
            self.act(XB[:, CT:TOK], PSA[0:RB, 0:T], AF.Copy)

        def emit_ro(grp):
            t0, nn, col, dc, n0 = grp
            ps = self.robank()
            for q in range(2):
                self.mm(ps[:, 0:nn], WO[:, q, dc * 128:(dc + 1) * 128], yr((n0 + q) % 3, t0, t0 + nn), q == 0, q == 1)
            self.stt(self.X[:, dc, t0:t0 + nn], ps[:, 0:nn], self.MODS[:, l, 16 + dc, col:col + 1],
                     self.X[:, dc, t0:t0 + nn], ALU.mult, ALU.add)

        prep_dg(0)
        prep_dg(1)
        for ti in range(5):
            xr_tile(0, ti)
        load_x(2)
        conv_mm(0)
        conv_ev()
        cast_xb()
        pending = []
        pending_wo = None
        for n in range(NRB):
            wg = WG[n % 2]
            ci = lambda d: (j * 2 + d) * 16 + n
            nxt = n + 1 < NRB
            for gi, (d, g) in enumerate(((1, 1), (0, 1), (1, 0), (0, 0))):
                dst = TT[d] if g == 1 else AA[d]
                bias = RC[:, 2 + g, ci(d):ci(d) + 1]
                for (t0, nn, isctx) in TILES:
                    ps = self.abank()
                    self.mm(ps[0:RB, 0:nn], wg[:, 2 * d + g, :], XB[:, t0:t0 + nn], True, True)
                    self.act(dst[:, t0:t0 + nn], ps[0:RB, 0:nn], AF.Tanh, bias=bias, scale=0.5)
                    for _ in range(2):
                        if pending and gi >= 2:
                            emit_ro(pending.pop(0))
                if g == 1:
                    pass
                else:
                    if gi == 2:
                        for dd in (1, 0):
                            self.stt(TT[dd][:, 0:CT], TT[dd][:, 0:CT], 1.0, XCc[:, :], ALU.add, ALU.mult)
                            self.stt(TT[dd][:, CT:TOK], TT[dd][:, CT:TOK], 1.0, PSA[0:RB, 0:T], ALU.add, ALU.mult)
                    coef = RC[:, 0, ci(d):ci(d) + 1]
                    hcoef = RC[:, 1, ci(d):ci(d) + 1]
                    self.act(MM[d][:, :], AA[d][:, :], AF.Exp, bias=coef, scale=coef)
                    self.act(AA[d][:, :], AA[d][:, :], AF.Exp, bias=hcoef, scale=hcoef)
                if nxt:
                    for ti in ((), (), (0, 1, 2), (3, 4))[gi]:
                        xr_tile(n + 1, ti)
                        for _ in range(2):
                            if pending:
                                emit_ro(pending.pop(0))
                    if gi == 3 and n + 3 < NRB:
                        load_x(n + 3)
            if n + 2 < NRB:
                load_wg(n + 2)
            if n % 2 == 1 or not nxt:
                while pending:
                    emit_ro(pending.pop(0))
                if pending_wo is not None:
                    loadwo(pending_wo)
                    pending_wo = None
            if nxt:
                conv_mm(n + 1)
            for d in (1, 0):
                self.act(MM[d][:, :], MM[d][:, :], AF.Sqrt, bias=0.25, scale=-0.25)
            if nxt:
                conv_ev()
                cast_xb()
            self.tt(TT[1][:, :], MM[1][:, :], TT[1][:, :], ALU.mult)
            self.scan(AA[1][:, CT - 1::-1], AA[1][:, CT - 1::-1], TT[1][:, CT - 1::-1], 0.0)
            self.scan(AA[1][:, TOK - 1:CT - 1:-1], AA[1][:, TOK - 1:CT - 1:-1], TT[1][:, TOK - 1:CT - 1:-1], AA[1][:, 0:1])
            self.tt(TT[0][:, :], MM[0][:, :], TT[0][:, :], ALU.mult)
            self.scan(MM[0][:, 0:CT], AA[0][:, 0:CT], TT[0][:, 0:CT], 0.0)
            self.scan(MM[0][:, CT:TOK], AA[0][:, CT:TOK], TT[0][:, CT:TOK], MM[0][:, CT - 1:CT])
            self.tt(MM[0][:, :], MM[0][:, :], AA[1][:, :], ALU.add)
            for (t0, nn, isctx) in TILES:
                if last and isctx:
                    continue
                ps = self.abank()
                for k in range(8):
                    self.mm(ps[0:RB, 0:nn], WING[n % 2][:, k, :], self.H[:, k, t0:t0 + nn], k == 0, k == 7)
                self.act(MM[1][:, t0:t0 + nn], ps[0:RB, 0:nn], AF.Gelu_apprx_tanh)
            lo = CT if last else 0
            self.tt(yr(n % 3, lo, TOK), MM[0][:, lo:TOK], MM[1][:, lo:TOK], ALU.mult)
            if n + 2 < NRB:
                load_g(n + 2)
                prep_dg(n + 2)
            if n % 2 == 1:
                for (t0, nn, isctx) in TILES:
                    if last and isctx:
                        continue
                    col = self.NB if isctx else b
                    for dc in range(8):
                        pending.append((t0, nn, col, dc, n - 1))
                if nxt:
                    pending_wo = n + 1
                else:
                    while pending:
                        emit_ro(pending.pop(0))

    def final(self, b):
        TMP, SQ, R = self.norm_bufs()
        STO = [self.abuf(f"STO{i}", 36864 + i * 4096, (128, 1024), F32, coarse=True) for i in range(2)]
        cnt = 0
        for i, (t0, n, isctx) in enumerate(TILES[1:]):
            self.rstd(t0, n, SQ[i % 2], R[i % 2])
            self.tt(TMP[:, :, 0:n], self.X[:, :, t0:t0 + n], R[i % 2][:, 0:n].bc_mid(8), ALU.mult)
            for c in range(8):
                if c < 4:
                    self.act(TMP[:, c, 0:n], TMP[:, c, 0:n], AF.Identity, scale=self.FG[:, c:c + 1])
                else:
                    self.ts(TMP[:, c, 0:n], TMP[:, c, 0:n], self.FG[:, c:c + 1], None, ALU.mult)
            for q in range(4):
                sto = STO[cnt % 2]
                pa, pb = (self.PS[0], self.PS[1]) if cnt % 2 == 0 else (self.PS[2], self.PS[3])
                for c in range(8):
                    pbk = pa if c < 4 else pb
                    self.transpose(pbk[:, (c % 4) * 128:(c % 4 + 1) * 128], TMP[:, c, q * 128:(q + 1) * 128], self.IDENT[:, :])
                self.act(View(sto, sto.ap[:, 0:512], sto.full), pa[:, :], AF.Copy)
                self.copy(View(sto, sto.ap[:, 512:1024], sto.full), pb[:, :])
                tok = t0 - CT + q * 128
                self.dma("sp", self.out[b, tok:tok + 128, :], sto[:, :], f"so{cnt % 2}")
                cnt += 1


N_CORES = 8


def _run(inputs, n_cores, NB, layers=(0, 1, 2, 3), do_mixer=True, do_mlp=True, trace=False):
    f32 = lambda a: np.ascontiguousarray(np.asarray(a), dtype=np.float32)
    x, c, ctx, c_ctx = f32(inputs["x"]), f32(inputs["c"]), f32(inputs["ctx"]), f32(inputs["c_ctx"])
    kb = K(NB, layers, do_mixer, do_mlp)
    nc = kb.build()
    consts = _host_consts()
    sv = _host_sv(f32(inputs["norm_g"]), f32(inputs["final_g"]), f32(inputs["b_mod"]), f32(inputs["b1"]),
                  f32(inputs["b2"]), f32(inputs["conv_w"]), f32(inputs["conv_b"]), f32(inputs["b_a"]),
                  f32(inputs["b_i"]), f32(inputs["lam"]))
    shared = dict(sv=sv, w_mod=f32(inputs["w_mod"]), w_fourier=f32(inputs["w_fourier"]),
                  w_rnn_in=f32(inputs["w_rnn_in"]), w_a=f32(inputs["w_a"]), w_i=f32(inputs["w_i"]),
                  w_rnn_out=f32(inputs["w_rnn_out"]), w1=f32(inputs["w1"]), w2=f32(inputs["w2"]),
                  conv_b=f32(inputs["conv_b"]), **consts)
    in_maps = []
    for i in range(n_cores):
        bs = slice(i * NB, (i + 1) * NB)
        call = np.concatenate([c[bs], c_ctx[None, :]], axis=0)
        cT = np.ascontiguousarray(call.reshape(NB + 1, 8, 128).transpose(2, 1, 0))
        m = dict(shared)
        m.update(x=np.ascontiguousarray(x[bs]), ctx=np.ascontiguousarray(ctx[bs]), cT=cT)
        in_maps.append(m)
    res = run_bass_kernel_spmd(nc, in_maps, core_ids=list(range(n_cores)), **({"trace": True} if trace else {}))
    out = np.concatenate([r["out"] for r in res.results], axis=0)
    return out, res


def kernel(**inputs):
    out, _ = _run(inputs, N_CORES, 4)
    return out.astype(np.float32)
```

```python
import math
from bisect import bisect_left
from contextlib import ExitStack

import numpy as np
import ml_dtypes
import concourse.bass as bass
import concourse.mybir as mybir
from concourse.bass_utils import run_bass_kernel_spmd

F32 = mybir.dt.float32
BF16 = mybir.dt.bfloat16
AF = mybir.ActivationFunctionType
ALU = mybir.AluOpType

D = 1024
T = 2048
CT = 256
TOK = T + CT
DFF = 4096
DR = 1344
RB = 84
NRB = 16
DEPTH = 4
EPS = 1e-6
ENGS = ("pe", "act", "dve", "pool", "sp")
TILES = [(0, 256, True)] + [(256 + 512 * i, 512, False) for i in range(4)]
AR_BYTES = 92672


def _ovl(a, b):
    for (l0, h0), (l1, h1) in zip(a, b):
        if h0 <= l1 or h1 <= l0:
            return False
    return True


def _inside(a, b):
    for (l0, h0), (l1, h1) in zip(a, b):
        if l0 < l1 or h0 > h1:
            return False
    return True


class Rec:
    __slots__ = ("box", "op", "w")

    def __init__(self, box, op, w):
        self.box, self.op, self.w = box, op, w


class Op:
    __slots__ = ("idx", "eng", "fn", "dma", "deps", "signal", "seq", "waits", "epoch")

    def __init__(self, idx, eng, fn, dma, epoch):
        self.idx, self.eng, self.fn, self.dma, self.epoch = idx, eng, fn, dma, epoch
        self.deps = ()
        self.signal = False
        self.seq = 0
        self.waits = ()


class View:
    __slots__ = ("buf", "ap", "box")

    def __init__(self, buf, ap, box):
        self.buf, self.ap, self.box = buf, ap, box

    def re(self, pat, **kw):
        return View(self.buf, self.ap.rearrange(pat, **kw), self.box)

    def bc_mid(self, n):
        p, f = self.ap.shape
        return View(self.buf, self.ap.unsqueeze(1).to_broadcast([p, n, f]), self.box)

    def bc_last(self, n):
        p, c = self.ap.shape
        return View(self.buf, self.ap.unsqueeze(2).to_broadcast([p, c, n]), self.box)


class Buf:
    def __init__(self, name, ap, shape, coarse=False, inherit=(), tracked=True):
        self.name, self.ap, self.shape = name, ap, tuple(shape)
        self.coarse = coarse
        self.recs = []
        self.inherit = list(inherit)
        self.tracked = tracked
        self.full = tuple((0, s) for s in self.shape)

    def __getitem__(self, idx):
        if not isinstance(idx, tuple):
            idx = (idx,)
        ap = self.ap[idx]
        if self.coarse:
            return View(self, ap, self.full)
        box = []
        for d in range(len(self.shape)):
            if d < len(idx):
                i = idx[d]
                if isinstance(i, int):
                    box.append((i, i + 1))
                else:
                    st, sp, step = i.indices(self.shape[d])
                    if step > 0:
                        box.append((st, sp))
                    else:
                        box.append((sp + 1, st + 1))
            else:
                box.append((0, self.shape[d]))
        return View(self, ap, tuple(box))


class Bank:
    def __init__(self, psa, i):
        self.psa, self.off = psa, 512 * i

    def __getitem__(self, idx):
        p, c = idx
        st, sp, _ = c.indices(512)
        return self.psa[p, self.off + st:self.off + sp]


class Prog:
    def __init__(self):
        self.ops = []
        self.epoch = 0

    def add(self, eng, fn, w=(), r=(), dma=None, extra=()):
        op = Op(len(self.ops), eng, fn, dma, self.epoch)
        deps = set(extra)
        for v in r:
            b = v.buf
            if b is None or not b.tracked:
                continue
            deps.update(b.inherit)
            for rec in b.recs:
                if rec.w and _ovl(rec.box, v.box):
                    deps.add(rec.op)
        for v in w:
            b = v.buf
            if b is None or not b.tracked:
                continue
            deps.update(b.inherit)
            for rec in b.recs:
                if _ovl(rec.box, v.box):
                    deps.add(rec.op)
        for v in r:
            b = v.buf
            if b is None or not b.tracked:
                continue
            if dma is None:
                b.recs = [x for x in b.recs if x.w or x.op.eng != eng or x.op.dma is not None
                          or not _inside(x.box, v.box)]
            b.recs.append(Rec(v.box, op, False))
        for v in w:
            b = v.buf
            if b is None or not b.tracked:
                continue
            b.recs = [x for x in b.recs if not _inside(x.box, v.box)]
            b.recs.append(Rec(v.box, op, True))
        deps.discard(op)
        op.deps = deps
        self.ops.append(op)
        return op

    @staticmethod
    def _skip(d, op):
        return d.dma is None and op.dma is None and d.eng == "pe" and op.eng == "pe"

    def finalize(self):
        for op in self.ops:
            for d in op.deps:
                if d.dma is None and not self._skip(d, op):
                    d.signal = True
        cnt = {}
        for op in self.ops:
            if op.dma is None and op.signal:
                k = (op.eng, op.epoch)
                cnt[k] = cnt.get(k, 0) + 1
                op.seq = cnt[k]
        self.keyops = {}
        for op in self.ops:
            if op.dma is not None:
                self.keyops.setdefault(op.dma, []).append(op.idx)
        known = {e: {} for e in ENGS}
        for op in self.ops:
            need = {}
            for d in op.deps:
                if d.dma is not None:
                    n = bisect_left(self.keyops[d.dma], d.idx) + 1
                    sem, val = ("dma", d.dma), 16 * n
                else:
                    if self._skip(d, op):
                        continue
                    sem, val = ("eng", d.eng, d.epoch), d.seq
                if need.get(sem, 0) < val:
                    need[sem] = val
            k = known[op.eng]
            waits = []
            for sem, val in need.items():
                if k.get(sem, 0) < val:
                    k[sem] = val
                    waits.append((sem, val))
            op.waits = waits
        self.engsems = sorted(cnt.keys())
        self.maxcnt = max(cnt.values()) if cnt else 0

    def emit(self, nc, st):
        sems = {}
        for (e, ep) in self.engsems:
            sems[("eng", e, ep)] = st.enter_context(nc.semaphore(f"s_{e}_{ep}"))
        for k in self.keyops:
            sems[("dma", k)] = st.enter_context(nc.semaphore(f"d_{k}"))
        block = st.enter_context(nc.Block())
        per = {e: [op for op in self.ops if op.eng == e] for e in ENGS}

        def run(name):
            def f(eng):
                for op in per[name]:
                    for sem, val in op.waits:
                        eng.wait_ge(sems[sem], val)
                    if op.fn is not None:
                        ins = op.fn(eng)
                        if op.dma is not None:
                            ins.then_inc(sems[("dma", op.dma)], 16)
                        elif op.signal:
                            ins.then_inc(sems[("eng", name, op.epoch)], 1)
            return f

        block.sync(run("sp"))
        block.tensor(run("pe"))
        block.scalar(run("act"))
        block.vector(run("dve"))
        block.gpsimd(run("pool"))


SV_LAYOUT = {}
_o = 0
for _n, _c in (("g", 64), ("fg", 8), ("bmod", 192), ("b1", 128), ("b2", 32), ("cw", 128), ("cb", 32),
               ("ba", 64), ("bi", 64), ("lam", 64)):
    SV_LAYOUT[_n] = (_o, _c)
    _o += _c
NSV = _o


def _host_sv(norm_g, final_g, b_mod, b1, b2, conv_w, conv_b, b_a, b_i, lam):
    sv = np.zeros((128, NSV), np.float32)

    def put(name, arr):
        o, c = SV_LAYOUT[name]
        a = np.ascontiguousarray(arr, dtype=np.float32).reshape(arr.shape[0], -1)
        assert a.shape[1] == c, (name, a.shape)
        sv[: a.shape[0], o:o + c] = a

    put("g", norm_g.reshape(4, 2, 8, 128).transpose(3, 0, 1, 2))
    put("fg", final_g.reshape(8, 128).transpose(1, 0))
    put("bmod", b_mod.reshape(4, 48, 128).transpose(2, 0, 1))
    put("b1", b1.reshape(4, 32, 128).transpose(2, 0, 1))
    put("b2", b2.reshape(4, 8, 128).transpose(2, 0, 1))
    put("cw", conv_w.reshape(2, 4, 16, 84).transpose(3, 0, 1, 2))
    put("cb", conv_b.reshape(2, 16, 84).transpose(2, 0, 1))
    put("ba", b_a.reshape(2, 2, 16, 84).transpose(3, 0, 1, 2))
    put("bi", b_i.reshape(2, 2, 16, 84).transpose(3, 0, 1, 2))
    put("lam", lam.reshape(2, 2, 16, 84).transpose(3, 0, 1, 2))
    return sv


def _host_consts():
    bf = ml_dtypes.bfloat16
    ident = np.eye(128, dtype=np.float32)
    ones = np.ones((128, 128), dtype=bf)
    c = np.arange(256, dtype=np.float64)[:, None]
    cp = np.arange(256, dtype=np.float64)[None, :]
    ang = 2 * np.pi * ((c * cp) % 256) / 256
    cs = np.concatenate([np.cos(ang), np.sin(ang)], axis=1) / 16.0
    csc = cs.reshape(2, 128, 512).transpose(1, 0, 2).astype(bf)
    t = np.arange(T, dtype=np.int64)[:, None]
    k = np.arange(T, dtype=np.int64)[None, :]
    ang = 2 * np.pi * ((t * k) % T).astype(np.float64) / T
    s = 1.0 / math.sqrt(T)
    cosm = (np.cos(ang) * s)[:, :T // 2].reshape(16, 128, 4, 256)
    sinm = (np.sin(ang) * s)[:, :T // 2].reshape(16, 128, 4, 256)
    dftx = np.stack([cosm, sinm], axis=0).transpose(3, 2, 0, 1, 4)
    dftx = np.ascontiguousarray(dftx).astype(bf)
    alt = np.zeros((128, 2), dtype=np.float64)
    alt[:, 0] = np.where(np.arange(128) % 2 == 0, 1.0, -1.0) / math.sqrt(T)
    alt = alt.astype(bf)
    t = np.arange(CT, dtype=np.int64)[:, None]
    k = np.arange(CT, dtype=np.int64)[None, :]
    ang = 2 * np.pi * ((t * k) % CT).astype(np.float64) / CT
    s = 1.0 / math.sqrt(CT)
    cosm = (np.cos(ang) * s).reshape(2, 128, 256)
    sinm = (-np.sin(ang) * s).reshape(2, 128, 256)
    dftc = np.ascontiguousarray(np.stack([cosm, sinm], axis=0).transpose(2, 0, 1, 3)).astype(bf)
    onesrow = np.ones((1, 2320), dtype=bf)
    return dict(ident=ident, ones=ones, csc=csc, dftx=dftx, dftc=dftc, onesrow=onesrow, alt=alt)


class K:
    def __init__(self, NB, layers=(0, 1, 2, 3), do_mixer=True, do_mlp=True):
        self.NB = NB
        self.layers = tuple(layers)
        self.do_mixer, self.do_mlp = do_mixer, do_mlp
        self.P = Prog()
        self.nc = bass.Bass("TRN2", target_bir_lowering=False)

    def mm(self, out, lhsT, rhs, start, stop):
        self.P.add("pe", lambda e: e.matmul(out.ap, lhsT.ap, rhs.ap, start=start, stop=stop),
                   w=[out], r=[lhsT, rhs])

    def transpose(self, out, in_, ident):
        self.P.add("pe", lambda e: e.transpose(out.ap, in_.ap, ident.ap), w=[out], r=[in_, ident])

    def act(self, out, in_, func, bias=None, scale=None):
        r = [in_]
        kw = {}
        if bias is not None:
            if isinstance(bias, View):
                r.append(bias)
                kw["bias"] = bias.ap
            else:
                kw["bias"] = float(bias)
        if scale is not None:
            if isinstance(scale, View):
                r.append(scale)
                kw["scale"] = scale.ap
            else:
                kw["scale"] = float(scale)
        self.P.add("act", lambda e: e.activation(out.ap, in_.ap, func, **kw), w=[out], r=r)

    def tt(self, out, in0, in1, op, eng="dve"):
        self.P.add(eng, lambda e: e.tensor_tensor(out.ap, in0.ap, in1.ap, op), w=[out], r=[in0, in1])

    def stt(self, out, in0, scalar, in1, op0, op1, eng="dve"):
        r = [in0, in1]
        if isinstance(scalar, View):
            r.append(scalar)
            sc = scalar.ap
        else:
            sc = float(scalar)
        self.P.add(eng, lambda e: e.scalar_tensor_tensor(out.ap, in0.ap, sc, in1.ap, op0, op1), w=[out], r=r)

    def ts(self, out, in0, s1, s2, op0, op1=None, eng="dve"):
        r = [in0]
        a1 = s1.ap if isinstance(s1, View) else float(s1)
        if isinstance(s1, View):
            r.append(s1)
        if s2 is None:
            self.P.add(eng, lambda e: e.tensor_scalar(out.ap, in0.ap, a1, None, op0), w=[out], r=r)
            return
        a2 = s2.ap if isinstance(s2, View) else float(s2)
        if isinstance(s2, View):
            r.append(s2)
        self.P.add(eng, lambda e: e.tensor_scalar(out.ap, in0.ap, a1, a2, op0, op1), w=[out], r=r)

    def copy(self, out, in_, eng="dve"):
        if eng == "act":
            self.act(out, in_, AF.Copy)
        else:
            self.P.add(eng, lambda e: e.tensor_copy(out.ap, in_.ap), w=[out], r=[in_])

    def scan(self, out, d0, d1, initial):
        r = [d0, d1]
        if isinstance(initial, View):
            r.append(initial)
            ini = initial.ap
        else:
            ini = float(initial)
        self.P.add("dve", lambda e: e.tensor_tensor_scan(out.ap, d0.ap, d1.ap, ini, ALU.mult, ALU.add),
                   w=[out], r=r)

    def dma(self, q, out, in_, key, w=None, r=None):
        ws = [out] if w is None else w
        rs = [in_] if r is None else r
        return self.P.add(q, lambda e: e.dma_start(out=out.ap, in_=in_.ap), w=ws, r=rs, dma=key)

    def abuf(self, name, off, shape, dt, coarse=False):
        esz = 4 if dt == F32 else 2
        n = 1
        for s in shape[1:]:
            n *= s
        nbytes = n * esz
        assert off % 4 == 0 and off + nbytes <= AR_BYTES, (name, off, nbytes)
        ap = self.ar_t[0:shape[0], off // 2:(off + nbytes) // 2]
        if dt == F32:
            ap = ap.bitcast(F32)
        if len(shape) == 3:
            ap = ap.rearrange("p (a b) -> p a b", a=shape[1])
        elif len(shape) == 4:
            ap = ap.rearrange("p (a b c) -> p a b c", a=shape[1], b=shape[2])
        inh = {}
        live = []
        for (o2, n2, b2) in self.ar_live:
            if o2 < off + nbytes and off < o2 + n2:
                for rec in b2.recs:
                    kk = ("dma", rec.op.dma) if rec.op.dma is not None else ("eng", rec.op.eng)
                    if kk not in inh or inh[kk].idx < rec.op.idx:
                        inh[kk] = rec.op
                for d in b2.inherit:
                    kk = ("dma", d.dma) if d.dma is not None else ("eng", d.eng)
                    if kk not in inh or inh[kk].idx < d.idx:
                        inh[kk] = d
                if not (off <= o2 and o2 + n2 <= off + nbytes):
                    live.append((o2, n2, b2))
            else:
                live.append((o2, n2, b2))
        b = Buf(name, ap, shape, coarse=coarse, inherit=list(inh.values()))
        live.append((off, nbytes, b))
        self.ar_live = live
        return b

    def build(self):
        nc, NB = self.nc, self.NB
        st = ExitStack()
        self.st = st
        dr = {}

        def din(name, shape, dt=F32):
            dr[name] = Buf(name, nc.dram_tensor(name, list(shape), dt, kind="ExternalInput").ap(), shape,
                           tracked=False)
            return dr[name]

        din("x", (NB, T, D))
        din("ctx", (NB, CT, D))
        din("cT", (128, 8, NB + 1))
        din("sv", (128, NSV))
        din("w_mod", (4, D, 6 * D))
        din("w_fourier", (2, D, D))
        din("w_rnn_in", (2, D, 2 * DR))
        din("w_a", (2, 2, NRB, RB, RB))
        din("w_i", (2, 2, NRB, RB, RB))
        din("w_rnn_out", (2, DR, D))
        din("w1", (4, D, DFF))
        din("w2", (4, DFF, D))
        din("ident", (128, 128))
        din("ones", (128, 128), BF16)
        din("csc", (128, 2, 512), BF16)
        din("dftx", (4, 128, 2, 16, 256), BF16)
        din("alt", (128, 2), BF16)
        din("dftc", (128, 2, 2, 256), BF16)
        din("conv_b", (2, DR))
        din("onesrow", (1, 2320), BF16)
        self.dr = dr
        self.out = Buf("out", nc.dram_tensor("out", [NB, T, D], F32, kind="ExternalOutput").ap(), (NB, T, D),
                       tracked=False)

        def sb(name, shape, dt):
            return st.enter_context(nc.sbuf_tensor(name, list(shape), dt))

        self.X = Buf("X", sb("X", (128, 8, TOK), F32)[:], (128, 8, TOK))
        self.H = Buf("H", sb("H", (128, 8, TOK), BF16)[:], (128, 8, TOK))
        self.ar_t = sb("AR", (128, AR_BYTES // 2), BF16)
        self.ar_live = []
        self.SV = Buf("SV", sb("SV", (128, NSV), F32)[:], (128, NSV), coarse=True)
        self.MODS = Buf("MODS", sb("MODS", (128, 4, 48, NB + 1), F32)[:], (128, 4, 48, NB + 1))
        self.SCAL = [Buf(f"SCAL{i}", sb(f"SCAL{i}", (128, 2, 3, 8), F32)[:], (128, 2, 3, 8)) for i in range(2)]
        self.RC = Buf("RC", sb("RC", (RB, 4, 64), F32)[:], (RB, 4, 64), coarse=True)
        self.IDENT = Buf("IDENT", sb("IDENT", (128, 128), F32)[:], (128, 128), coarse=True)
        self.ONES = Buf("ONES", sb("ONES", (128, 128), BF16)[:], (128, 128), coarse=True)
        self.CTs = Buf("CTs", sb("CTs", (128, 8, NB + 1), F32)[:], (128, 8, NB + 1), coarse=True)
        self.SCb = Buf("SCb", sb("SCb", (128, 8, NB + 1), BF16)[:], (128, 8, NB + 1), coarse=True)
        pt = st.enter_context(nc.psum_tensor("PSA", [128, 4096], F32))
        self.PSA = Buf("PSA", pt[:], (128, 4096))
        self.PS = [Bank(self.PSA, i) for i in range(8)]
        self.rot = 0

        self.stage_setup()
        cnt = 0
        for b in range(NB):
            self.P.epoch = b
            self.load_x(b)
            for l in self.layers:
                par = cnt % 2
                cnt += 1
                self.compute_scal(b, l, par)
                if self.do_mixer:
                    if l % 2 == 0:
                        self.fourier(b, l, par)
                    else:
                        self.rnn(b, l, par)
                if self.do_mlp:
                    self.mlp(b, l, par)
            self.final(b)
        last = {}
        for op in self.P.ops:
            if op.dma is not None and op.dma.startswith("so"):
                last[op.dma] = op
        self.P.add("sp", None, extra=list(last.values()))
        self.P.finalize()
        self.P.emit(nc, st)
        st.close()
        return nc

    def svv(self, name, pat=None, np_=128, **kw):
        o, c = SV_LAYOUT[name]
        ap = self.SV.ap[0:np_, o:o + c]
        if pat is not None:
            ap = ap.rearrange(pat, **kw)
        return ap

    def stage_setup(self):
        NB, dr = self.NB, self.dr
        d_sv = self.dma("sp", self.SV[:, :], dr["sv"][:, :], "sv")
        self.dma("sp", self.IDENT[:, :], dr["ident"][:, :], "ident")
        self.dma("sp", self.ONES[:, :], dr["ones"][:, :], "ones")
        self.dma("sp", self.CTs[:, :, :], dr["cT"][:, :, :], "cT")
        inh = [d_sv]
        self.G = Buf("G", self.svv("g", "p (l n k) -> p l n k", l=4, n=2), (128, 4, 2, 8), inherit=inh)
        self.FG = Buf("FG", self.svv("fg"), (128, 8), inherit=inh)
        self.BMOD = Buf("BMOD", self.svv("bmod", "p (l j) -> p l j", l=4), (128, 4, 48), inherit=inh)
        self.B1 = Buf("B1", self.svv("b1", "p (l f) -> p l f", l=4), (128, 4, 32), inherit=inh)
        self.B2 = Buf("B2", self.svv("b2", "p (l k) -> p l k", l=4), (128, 4, 8), inherit=inh)
        self.CW = Buf("CW", self.svv("cw", "p (j k n) -> p j k n", np_=RB, j=2, k=4), (RB, 2, 4, NRB), inherit=inh)
        self.CB = Buf("CB", self.svv("cb", "p (j n) -> p j n", np_=RB, j=2), (RB, 2, NRB), inherit=inh)
        self.BA = Buf("BA", self.svv("ba", "p (j d n) -> p j d n", np_=RB, j=2, d=2), (RB, 2, 2, NRB), inherit=inh)
        self.BI = Buf("BI", self.svv("bi", "p (j d n) -> p j d n", np_=RB, j=2, d=2), (RB, 2, 2, NRB), inherit=inh)
        self.LAM = Buf("LAM", self.svv("lam", "p (j d n) -> p j d n", np_=RB, j=2, d=2), (RB, 2, 2, NRB), inherit=inh)
        RC = self.RC
        LAMF = Buf("LAMF", self.svv("lam", np_=RB), (RB, 64), inherit=inh)
        BAF = Buf("BAF", self.svv("ba", np_=RB), (RB, 64), inherit=inh)
        BIF = Buf("BIF", self.svv("bi", np_=RB), (RB, 64), inherit=inh)
        self.act(RC[:, 0, :], LAMF[:, :], AF.Exp, scale=-1.0)
        self.act(RC[:, 0, :], RC[:, 0, :], AF.Ln, bias=1.0)
        self.ts(RC[:, 1, :], RC[:, 0, :], -4.0, None, ALU.mult)
        self.ts(RC[:, 0, :], RC[:, 0, :], -8.0, None, ALU.mult)
        self.ts(RC[:, 2, :], BAF[:, :], 0.5, None, ALU.mult)
        self.ts(RC[:, 3, :], BIF[:, :], 0.5, None, ALU.mult)
        self.act(self.SCb[:, :, :], self.CTs[:, :, :], AF.Silu)
        ncol = NB + 1
        WM = [self.abuf(f"WM{i}", i * 12288, (128, 8, 768), BF16, coarse=True) for i in range(2)]
        it = 0
        for l in self.layers:
            for g in range(8):
                wm = WM[it % 2]
                self.dma("pool", wm[:, :, :],
                         View(None, dr["w_mod"].ap[l][:, g * 768:(g + 1) * 768].rearrange("(k p) n -> p k n", p=128), None),
                         f"wm{it % 2}")
                ps = self.PS[it % 2]
                for jj in range(6):
                    for k in range(8):
                        self.mm(ps[:, jj * 8:jj * 8 + ncol], wm[:, k, jj * 128:(jj + 1) * 128], self.SCb[:, k, :],
                                k == 0, k == 7)
                v48 = self.PS[it % 2][:, 0:48]
                psv = View(v48.buf, v48.ap.rearrange("p (a b) -> p a b", a=6)[:, :, 0:ncol], v48.box)
                self.tt(self.MODS[:, l, g * 6:(g + 1) * 6, :], psv,
                        self.BMOD[:, l, g * 6:(g + 1) * 6].bc_last(ncol), ALU.add)
                it += 1

    def compute_scal(self, b, l, par):
        SC = self.SCAL[par]
        for s, col in ((0, b), (1, self.NB)):
            self.stt(SC[:, s, 0, :], self.MODS[:, l, 8:16, col], 1.0, self.G[:, l, 0, :], ALU.add, ALU.mult)
            self.stt(SC[:, s, 1, :], self.MODS[:, l, 32:40, col], 1.0, self.G[:, l, 1, :], ALU.add, ALU.mult)
            self.tt(SC[:, s, 2, :], self.MODS[:, l, 40:48, col], self.B2[:, l, :], ALU.mult)

    def load_x(self, b):
        dr = self.dr
        STG = [self.abuf(f"STG{i}", i * 4096, (128, 1024), F32, coarse=True) for i in range(2)]
        for tc in range(18):
            stg = STG[tc % 2]
            if tc < 2:
                src = dr["ctx"][b, tc * 128:(tc + 1) * 128, :]
            else:
                src = dr["x"][b, (tc - 2) * 128:(tc - 1) * 128, :]
            self.dma("sp", stg[:, :], src, f"stg{tc % 2}")
            pa, pb = (self.PS[0], self.PS[1]) if tc % 2 == 0 else (self.PS[2], self.PS[3])
            for j in range(8):
                pbk = pa if j < 4 else pb
                self.transpose(pbk[:, (j % 4) * 128:(j % 4 + 1) * 128], stg[:, j * 128:(j + 1) * 128], self.IDENT[:, :])
            self.act(self.X[:, 0:4, tc * 128:(tc + 1) * 128], pa[:, :].re("p (a b) -> p a b", a=4), AF.Copy)
            self.copy(self.X[:, 4:8, tc * 128:(tc + 1) * 128], pb[:, :].re("p (a b) -> p a b", a=4))

    def norm_bufs(self):
        TMP = self.abuf("NTMP", 0, (128, 8, 512), F32)
        SQ = [self.abuf(f"NSQ{i}", 16384 + i * 8192, (128, 8, 512), BF16) for i in range(2)]
        R = [self.abuf(f"NR{i}", 32768 + i * 2048, (128, 512), F32) for i in range(2)]
        return TMP, SQ, R

    def rstd(self, t0, n, SQ, R, part=3):
        ps = self.PS[7]
        if part & 1:
            self.act(SQ[:, :, 0:n], self.X[:, :, t0:t0 + n], AF.Square)
            for c in range(8):
                self.mm(ps[:, 0:n], self.ONES[:, :], SQ[:, c, 0:n], c == 0, c == 7)
        if part & 2:
            self.act(R[:, 0:n], ps[:, 0:n], AF.Ln, scale=1.0 / D, bias=EPS)
            self.act(R[:, 0:n], R[:, 0:n], AF.Exp, scale=-0.5)

    def norm_mod(self, b, l, which, tiles, par, bufs=None):
        TMP, SQ, R = bufs if bufs is not None else self.norm_bufs()
        SC = self.SCAL[par]

        def stage_a(i, part=3):
            t0, n, isctx = tiles[i]
            self.rstd(t0, n, SQ[i % 2], R[i % 2], part)

        def stage_b(i):
            t0, n, isctx = tiles[i]
            self.tt(TMP[:, :, 0:n], self.X[:, :, t0:t0 + n], R[i % 2][:, 0:n].bc_mid(8), ALU.mult)

        def stage_c(i):
            t0, n, isctx = tiles[i]
            s, col = (1, self.NB) if isctx else (0, b)
            for c in range(8):
                A = SC[:, s, which, c:c + 1]
                B = self.MODS[:, l, (24 if which else 0) + c, col:col + 1]
                if c < 4:
                    self.act(self.H[:, c, t0:t0 + n], TMP[:, c, 0:n], AF.Identity, bias=B, scale=A)
                else:
                    self.ts(self.H[:, c, t0:t0 + n], TMP[:, c, 0:n], A, B, ALU.mult, ALU.add)

        stage_a(0)
        for i in range(len(tiles)):
            stage_b(i)
            dbl = SQ[0] is not SQ[1]
            if i + 1 < len(tiles) and dbl:
                stage_a(i + 1, 1)
            stage_c(i)
            if i + 1 < len(tiles):
                stage_a(i + 1, 2 if dbl else 3)

    def mlp(self, b, l, par):
        dr = self.dr
        tiles = TILES if l < DEPTH - 1 else TILES[1:]
        W1S = [self.abuf(f"W1S{i}", 53248 + i * 8192, (128, 8, 512), BF16, coarse=True) for i in range(2)]
        W2S = [self.abuf(f"W2S{i}", 69632 + i * 8192, (128, 4, 1024), BF16, coarse=True) for i in range(2)]

        def load(g):
            self.dma("pool", W1S[g % 2][:, :, :],
                     View(None, dr["w1"].ap[l][:, g * 512:(g + 1) * 512].rearrange("(k p) n -> p k n", p=128), None),
                     f"w1s{g % 2}")
            self.dma("pool", W2S[g % 2][:, :, :],
                     View(None, dr["w2"].ap[l][g * 512:(g + 1) * 512, :].rearrange("(j p) n -> p j n", p=128), None),
                     f"w2s{g % 2}")

        load(0)
        load(1)
        nb_tmp = self.abuf("NTMP", 0, (128, 8, 512), F32)
        nb_sq = self.abuf("NSQ", 16384, (128, 8, 512), BF16)
        nb_r = self.abuf("NR", 24576, (128, 512), F32)
        nbufs = (nb_tmp, [nb_sq, nb_sq], [nb_r, nb_r])
        SC = self.SCAL[par]

        def prep_tile(ti):
            t0, n, isctx = tiles[ti]
            self.norm_mod(b, l, 1, [tiles[ti]], par, bufs=nbufs)
            s = 1 if isctx else 0
            self.tt(self.X[:, :, t0:t0 + n], self.X[:, :, t0:t0 + n], SC[:, s, 2, :].bc_last(n), ALU.add)

        RL = [self.abuf(f"RL{i}", 26624 + i * 8192, (128, 4, 512), F32) for i in range(2)]
        Z = [self.abuf(f"Z{i}", 43008 + i * 4096, (128, 4, 512), BF16) for i in range(2)]
        steps = [(g, ti) for g in range(8) for ti in range(len(tiles))]

        def h1(i):
            g, ti = steps[i]
            t0, n, isctx = tiles[ti]
            s = i % 2
            for j in range(4):
                ps = self.PS[j]
                for k in range(8):
                    self.mm(ps[:, 0:n], W1S[g % 2][:, k, j * 128:(j + 1) * 128], self.H[:, k, t0:t0 + n], k == 0, k == 7)
                self.act(RL[s][:, j, 0:n], ps[:, 0:n], AF.Relu, bias=self.B1[:, l, g * 4 + j:g * 4 + j + 1])
            self.act(Z[s][:, :, 0:n], RL[s][:, :, 0:n], AF.Square)

        def y(i):
            g, ti = steps[i]
            t0, n, isctx = tiles[ti]
            s = i % 2
            col = self.NB if isctx else b
            for dc in range(8):
                ps = self.PS[4 + dc % 2]
                for j in range(4):
                    self.mm(ps[:, 0:n], W2S[g % 2][:, j, dc * 128:(dc + 1) * 128], Z[s][:, j, 0:n], j == 0, j == 3)
                self.stt(self.X[:, dc, t0:t0 + n], ps[:, 0:n], self.MODS[:, l, 40 + dc, col:col + 1],
                         self.X[:, dc, t0:t0 + n], ALU.mult, ALU.add)

        prep_tile(0)
        h1(0)
        for i in range(len(steps)):
            if i + 1 < len(steps):
                if i + 1 < len(tiles):
                    prep_tile(i + 1)
                h1(i + 1)
            y(i)
            g, ti = steps[i]
            if ti == 0 and 1 <= g < 7:
                load(g + 1)

    def fourier(self, b, l, par):
        dr = self.dr
        j = l // 2
        CSC = self.abuf("CSC", 69632, (128, 2, 512), BF16, coarse=True)
        DFTC = self.abuf("DFTC", 71680, (128, 2, 2, 256), BF16, coarse=True)
        self.dma("sp", CSC[:, :, :], dr["csc"][:, :, :], "csc")
        self.dma("sp", DFTC[:, :, :, :], dr["dftc"][:, :, :, :], "dftc")
        ALT = self.abuf("ALT", 73728, (128, 2), BF16, coarse=True)
        self.dma("sp", ALT[:, :], dr["alt"][:, :], "alt")
        QT = [self.abuf(f"QT{i}", 73732 + i * 1024, (128, 256), F32) for i in range(2)]
        self.norm_mod(b, l, 0, TILES, par)
        UV = self.abuf("UV", 0, (128, 18, 2, 512), BF16)
        TAB = [self.abuf(f"TAB{i}", 36864 + i * 16384, (128, 2, 16, 256), BF16, coarse=True) for i in range(2)]
        ev = 0
        nt = 0
        for half in range(2):
            for tc in range(18):
                for gl in range(2):
                    g = 2 * half + gl
                    ps = self.PS[(tc * 2 + gl) % 2]
                    self.mm(ps[:, 0:512], self.H[:, 2 * g, tc * 128:(tc + 1) * 128], CSC[:, 0, :], True, False)
                    self.mm(ps[:, 0:512], self.H[:, 2 * g + 1, tc * 128:(tc + 1) * 128], CSC[:, 1, :], False, True)
                    self.copy(UV[:, tc, gl, :], ps[:, 0:512], eng=("act" if ev % 2 == 0 else "dve"))
                    ev += 1
            for kt in range(4):
                tab = TAB[nt % 2]
                self.dma("sp", tab[:, :, :, :], dr["dftx"][kt], f"tab{nt % 2}")
                nt += 1
                k0 = kt * 256
                for ccl in range(4):
                    cc = 4 * half + ccl
                    gl, co = ccl // 2, (ccl % 2) * 128
                    ps = self.PS[2 + ev % 2]
                    qt = QT[ev % 2]
                    for tcx in range(16):
                        self.mm(ps[:, 0:256], UV[:, 2 + tcx, gl, co:co + 128], tab[:, 0, tcx, :], tcx == 0, tcx == 15)
                    for tcx in range(16):
                        self.mm(ps[:, 256:512], UV[:, 2 + tcx, gl, 256 + co:256 + co + 128], tab[:, 1, tcx, :], tcx == 0, tcx == 15)
                    self.act(qt[:, :], ps[:, 256:512], AF.Copy)
                    self.tt(self.H[:, cc, CT + k0:CT + k0 + 256], ps[:, 0:256], qt[:, :], ALU.subtract)
                    if kt == 0:
                        self.tt(self.H[:, cc, CT + T - 1:CT + T - 256:-1], ps[:, 1:256], qt[:, 1:256], ALU.add)
                    else:
                        self.tt(self.H[:, cc, CT + T - k0:CT + T - k0 - 256:-1], ps[:, 0:256], qt[:, :], ALU.add)
                    ev += 1
            for ccl in range(4):
                cc = 4 * half + ccl
                gl, co = ccl // 2, (ccl % 2) * 128
                ps = self.PS[2 + ev % 2]
                for tcx in range(16):
                    self.mm(ps[:, 0:1], UV[:, 2 + tcx, gl, co:co + 128], ALT[:, 0:1], tcx == 0, tcx == 15)
                self.act(self.H[:, cc, CT + T // 2:CT + T // 2 + 1], ps[:, 0:1], AF.Copy)
                ev += 1
            for ccl in range(4):
                cc = 4 * half + ccl
                gl, co = ccl // 2, (ccl % 2) * 128
                ps = self.PS[2 + ev % 2]
                for tcx in range(2):
                    self.mm(ps[:, 0:256], UV[:, tcx, gl, co:co + 128], DFTC[:, 0, tcx, :], tcx == 0, False)
                    self.mm(ps[:, 0:256], UV[:, tcx, gl, 256 + co:256 + co + 128], DFTC[:, 1, tcx, :], False, tcx == 1)
                self.copy(self.H[:, cc, 0:256], ps[:, 0:256], eng=("act" if ev % 2 == 0 else "dve"))
                ev += 1
        WF = self.abuf("WF", 36864, (128, 8, 1024), BF16, coarse=True)
        self.dma("pool", WF[:, :, :],
                 View(None, dr["w_fourier"].ap[j].rearrange("(k p) n -> p k n", p=128), None), "wf")
        for (t0, n, isctx) in TILES:
            col = self.NB if isctx else b
            for dc in range(8):
                ps = self.PS[4 + dc % 2]
                for k in range(8):
                    self.mm(ps[:, 0:n], WF[:, k, dc * 128:(dc + 1) * 128], self.H[:, k, t0:t0 + n], k == 0, k == 7)
                self.stt(self.X[:, dc, t0:t0 + n], ps[:, 0:n], self.MODS[:, l, 16 + dc, col:col + 1],
                         self.X[:, dc, t0:t0 + n], ALU.mult, ALU.add)

    def abank(self):
        self.rot = (self.rot + 1) % 2
        return self.PS[4 + self.rot]

    def robank(self):
        self.rot2 = (getattr(self, "rot2", 0) + 1) % 2
        return self.PS[6 + self.rot2]

    def rnn(self, b, l, par):
        dr = self.dr
        j = l // 2
        last = (l == DEPTH - 1)
        o = 64576
        XB = self.abuf("XB", o, (RB, TOK), BF16); o += 4608
        WINX, WING = [], []
        for i in range(2):
            WINX.append(self.abuf(f"WINX{i}", o, (128, 8, RB), BF16, coarse=True)); o += 1344
            WING.append(self.abuf(f"WING{i}", o, (128, 8, RB), BF16, coarse=True)); o += 1344
        WG = []
        for i in range(2):
            WG.append(self.abuf(f"WG{i}", o, (RB, 4, RB), BF16, coarse=True)); o += 672
        YR2 = self.abuf("YR2", o, (RB, 2, TOK), BF16); o += 9216
        WO = self.abuf("WO", o, (RB, 2, 1024), BF16, coarse=True); o += 4096
        o_tail = o
        assert o + 2368 <= AR_BYTES, o

        wsrc = dr["w_rnn_in"].ap[j]

        def load_g(n):
            self.dma("pool", WING[n % 2][:, :, :],
                     View(None, wsrc[:, RB * n:RB * (n + 1)].rearrange("(k p) n -> p k n", p=128), None), f"wing{n % 2}")

        def load_x(n):
            self.dma("pool", WINX[n % 2][:, :, :],
                     View(None, wsrc[:, DR + RB * n:DR + RB * (n + 1)].rearrange("(k p) n -> p k n", p=128), None),
                     f"winx{n % 2}")

        def load_wg(n):
            wg = WG[n % 2]
            for d in range(2):
                self.dma("pool", View(wg, wg.ap[:, 2 * d, :], wg.full), View(None, dr["w_a"].ap[j, d, n], None), f"wg{n % 2}")
                self.dma("pool", View(wg, wg.ap[:, 2 * d + 1, :], wg.full), View(None, dr["w_i"].ap[j, d, n], None), f"wg{n % 2}")

        def loadwo(n):
            self.dma("pool", WO[:, :, :],
                     View(None, dr["w_rnn_out"].ap[j][RB * n:RB * (n + 2), :].rearrange("(q p) d -> p q d", p=RB), None),
                     "wo")

        for i in range(2):
            load_x(i)
            load_wg(i)
            load_g(i)
        loadwo(0)
        self.norm_mod(b, l, 0, TILES, par)
        XRb = self.abuf("XRb", 0, (RB + 1, 2320), BF16)
        YR3 = self.abuf("YR3", 4640, (RB, 1, TOK), BF16)
        DG = [self.abuf(f"DG{i}", o_tail + i * 672, (RB + 1, 4, RB), BF16) for i in range(2)]
        XCc = self.abuf("XCc", o_tail + 1344, (RB, CT), F32)
        AA = [self.abuf(f"A{d}", 9280 + 27648 * d, (RB, TOK), F32) for d in range(2)]
        MM = [self.abuf(f"M{d}", 18496 + 27648 * d, (RB, TOK), F32) for d in range(2)]
        TT = [self.abuf(f"T{d}", 27712 + 27648 * d, (RB, TOK), F32) for d in range(2)]

        def yr(slot, lo, hi):
            return YR2[:, slot, lo:hi] if slot < 2 else YR3[:, 0, lo:hi]
        RC = self.RC
        PSA = self.PSA
        v = XRb[:, :]
        self.P.add("dve", lambda e: e.memset(v.ap, 0.0), w=[v])
        self.dma("sp", XRb[RB:RB + 1, :], dr["onesrow"][:, :], "onesrow")
        pcol = lambda t0: (t0 - CT) if t0 >= CT else T
        xcol = lambda t0: (261 + t0 - CT) if t0 >= CT else (2 + t0)

        def prep_dg(n):
            dg = DG[n % 2]
            for kk in range(4):
                self.ts(dg[0:RB, kk, :], self.IDENT[0:RB, 0:RB], self.CW[:, j, kk, n:n + 1], None, ALU.mult)
            self.dma("pool", dg[RB:RB + 1, 2, :], dr["conv_b"][j:j + 1, RB * n:RB * (n + 1)], f"dgb{n % 2}")

        def xr_tile(n, ti):
            t0, nn, isctx = TILES[ti]
            ps = self.abank()
            for k in range(8):
                self.mm(ps[0:RB, 0:nn], WINX[n % 2][:, k, :], self.H[:, k, t0:t0 + nn], k == 0, k == 7)
            c0 = xcol(t0)
            self.act(XRb[0:RB, c0:c0 + nn], ps[0:RB, 0:nn], AF.Copy)

        cps = [None]

        def conv_mm(n):
            dg = DG[n % 2]
            cps[0] = self.abank()
            for (t0, nn, isctx) in TILES:
                c0 = xcol(t0)
                dst = cps[0][0:RB, 0:nn] if isctx else PSA[0:RB, t0 - CT:t0 - CT + nn]
                for kk in range(4):
                    kp = RB + 1 if kk == 2 else RB
                    self.mm(dst, dg[0:kp, kk, :], XRb[0:kp, c0 + kk - 2:c0 + kk - 2 + nn], kk == 0, kk == 3)

        def conv_ev():
            self.act(XCc[:, :], cps[0][0:RB, 0:CT], AF.Copy)

        def cast_xb():
            self.act(XB[:, 0:CT], XCc[:, :], AF.Copy)
            self.act(XB[:, CT:TOK], PSA[0:RB, 0:T], AF.Copy)

        def emit_ro(grp):
            t0, nn, col, dc, n0 = grp
            ps = self.robank()
            for q in range(2):
                self.mm(ps[:, 0:nn], WO[:, q, dc * 128:(dc + 1) * 128], yr((n0 + q) % 3, t0, t0 + nn), q == 0, q == 1)
            self.stt(self.X[:, dc, t0:t0 + nn], ps[:, 0:nn], self.MODS[:, l, 16 + dc, col:col + 1],
                     self.X[:, dc, t0:t0 + nn], ALU.mult, ALU.add)

        prep_dg(0)
        prep_dg(1)
        for ti in range(5):
            xr_tile(0, ti)
        load_x(2)
        conv_mm(0)
        conv_ev()
        cast_xb()
        pending = []
        pending_wo = None
        for n in range(NRB):
            wg = WG[n % 2]
            ci = lambda d: (j * 2 + d) * 16 + n
            nxt = n + 1 < NRB
            for gi, (d, g) in enumerate(((1, 1), (0, 1), (1, 0), (0, 0))):
                dst = TT[d] if g == 1 else AA[d]
                bias = RC[:, 2 + g, ci(d):ci(d) + 1]
                for (t0, nn, isctx) in TILES:
                    ps = self.abank()
                    self.mm(ps[0:RB, 0:nn], wg[:, 2 * d + g, :], XB[:, t0:t0 + nn], True, True)
                    self.act(dst[:, t0:t0 + nn], ps[0:RB, 0:nn], AF.Tanh, bias=bias, scale=0.5)
                    if pending and gi >= 2:
                        emit_ro(pending.pop(0))
                if g == 1:
                    pass
                else:
                    if gi == 2:
                        for dd in (1, 0):
                            self.stt(TT[dd][:, 0:CT], TT[dd][:, 0:CT], 1.0, XCc[:, :], ALU.add, ALU.mult)
                            self.stt(TT[dd][:, CT:TOK], TT[dd][:, CT:TOK], 1.0, PSA[0:RB, 0:T], ALU.add, ALU.mult)
                    coef = RC[:, 0, ci(d):ci(d) + 1]
                    hcoef = RC[:, 1, ci(d):ci(d) + 1]
                    self.act(MM[d][:, :], AA[d][:, :], AF.Exp, bias=coef, scale=coef)
                    self.act(AA[d][:, :], AA[d][:, :], AF.Exp, bias=hcoef, scale=hcoef)
                if nxt:
                    for ti in ((), (), (0, 1, 2), (3, 4))[gi]:
                        xr_tile(n + 1, ti)
                        for _ in range(2):
                            if pending:
                                emit_ro(pending.pop(0))
                    if gi == 3 and n + 3 < NRB:
                        load_x(n + 3)
            if n + 2 < NRB:
                load_wg(n + 2)
            if n % 2 == 1 or not nxt:
                while pending:
                    emit_ro(pending.pop(0))
                if pending_wo is not None:
                    loadwo(pending_wo)
                    pending_wo = None
            if nxt:
                conv_mm(n + 1)
            for d in (1, 0):
                self.act(MM[d][:, :], MM[d][:, :], AF.Sqrt, bias=0.25, scale=-0.25)
            if nxt:
                conv_ev()
                cast_xb()
            self.tt(TT[1][:, :], MM[1][:, :], TT[1][:, :], ALU.mult)
            self.scan(AA[1][:, CT - 1::-1], AA[1][:, CT - 1::-1], TT[1][:, CT - 1::-1], 0.0)
            self.scan(AA[1][:, TOK - 1:CT - 1:-1], AA[1][:, TOK - 1:CT - 1:-1], TT[1][:, TOK - 1:CT - 1:-1], AA[1][:, 0:1])
            self.tt(TT[0][:, :], MM[0][:, :], TT[0][:, :], ALU.mult)
            self.scan(MM[0][:, 0:CT], AA[0][:, 0:CT], TT[0][:, 0:CT], 0.0)
            self.scan(MM[0][:, CT:TOK], AA[0][:, CT:TOK], TT[0][:, CT:TOK], MM[0][:, CT - 1:CT])
            self.tt(MM[0][:, :], MM[0][:, :], AA[1][:, :], ALU.add)
            for (t0, nn, isctx) in TILES:
                if last and isctx:
                    continue
                ps = self.abank()
                for k in range(8):
                    self.mm(ps[0:RB, 0:nn], WING[n % 2][:, k, :], self.H[:, k, t0:t0 + nn], k == 0, k == 7)
                self.act(MM[1][:, t0:t0 + nn], ps[0:RB, 0:nn], AF.Gelu_apprx_tanh)
            lo = CT if last else 0
            self.tt(yr(n % 3, lo, TOK), MM[0][:, lo:TOK], MM[1][:, lo:TOK], ALU.mult)
            if n + 2 < NRB:
                load_g(n + 2)
                prep_dg(n + 2)
            if n % 2 == 1:
                for (t0, nn, isctx) in TILES:
                    if last and isctx:
                        continue
                    col = self.NB if isctx else b
                    for dc in range(8):
                        pending.append((t0, nn, col, dc, n - 1))
                if nxt:
                    pending_wo = n + 1
                else:
                    while pending:
                        emit_ro(pending.pop(0))

    def final(self, b):
        TMP, SQ, R = self.norm_bufs()
        STO = [self.abuf(f"STO{i}", 36864 + i * 4096, (128, 1024), F32, coarse=True) for i in range(2)]
        cnt = 0
        for i, (t0, n, isctx) in enumerate(TILES[1:]):
            self.rstd(t0, n, SQ[i % 2], R[i % 2])
            self.tt(TMP[:, :, 0:n], self.X[:, :, t0:t0 + n], R[i % 2][:, 0:n].bc_mid(8), ALU.mult)
            for c in range(8):
                if c < 4:
                    self.act(TMP[:, c, 0:n], TMP[:, c, 0:n], AF.Identity, scale=self.FG[:, c:c + 1])
                else:
                    self.ts(TMP[:, c, 0:n], TMP[:, c, 0:n], self.FG[:, c:c + 1], None, ALU.mult)
            for q in range(4):
                sto = STO[cnt % 2]
                pa, pb = (self.PS[0], self.PS[1]) if cnt % 2 == 0 else (self.PS[2], self.PS[3])
                for c in range(8):
                    pbk = pa if c < 4 else pb
                    self.transpose(pbk[:, (c % 4) * 128:(c % 4 + 1) * 128], TMP[:, c, q * 128:(q + 1) * 128], self.IDENT[:, :])
                self.act(View(sto, sto.ap[:, 0:512], sto.full), pa[:, :], AF.Copy)
                self.copy(View(sto, sto.ap[:, 512:1024], sto.full), pb[:, :])
                tok = t0 - CT + q * 128
                self.dma("sp", self.out[b, tok:tok + 128, :], sto[:, :], f"so{cnt % 2}")
                cnt += 1


N_CORES = 8


def _run(inputs, n_cores, NB, layers=(0, 1, 2, 3), do_mixer=True, do_mlp=True, trace=False):
    f32 = lambda a: np.ascontiguousarray(np.asarray(a), dtype=np.float32)
    x, c, ctx, c_ctx = f32(inputs["x"]), f32(inputs["c"]), f32(inputs["ctx"]), f32(inputs["c_ctx"])
    kb = K(NB, layers, do_mixer, do_mlp)
    nc = kb.build()
    consts = _host_consts()
    sv = _host_sv(f32(inputs["norm_g"]), f32(inputs["final_g"]), f32(inputs["b_mod"]), f32(inputs["b1"]),
                  f32(inputs["b2"]), f32(inputs["conv_w"]), f32(inputs["conv_b"]), f32(inputs["b_a"]),
                  f32(inputs["b_i"]), f32(inputs["lam"]))
    shared = dict(sv=sv, w_mod=f32(inputs["w_mod"]), w_fourier=f32(inputs["w_fourier"]),
                  w_rnn_in=f32(inputs["w_rnn_in"]), w_a=f32(inputs["w_a"]), w_i=f32(inputs["w_i"]),
                  w_rnn_out=f32(inputs["w_rnn_out"]), w1=f32(inputs["w1"]), w2=f32(inputs["w2"]),
                  conv_b=f32(inputs["conv_b"]), **consts)
    in_maps = []
    for i in range(n_cores):
        bs = slice(i * NB, (i + 1) * NB)
        call = np.concatenate([c[bs], c_ctx[None, :]], axis=0)
        cT = np.ascontiguousarray(call.reshape(NB + 1, 8, 128).transpose(2, 1, 0))
        m = dict(shared)
        m.update(x=np.ascontiguousarray(x[bs]), ctx=np.ascontiguousarray(ctx[bs]), cT=cT)
        in_maps.append(m)
    res = run_bass_kernel_spmd(nc, in_maps, core_ids=list(range(n_cores)), **({"trace": True} if trace else {}))
    out = np.concatenate([r["out"] for r in res.results], axis=0)
    return out, res


def kernel(**inputs):
    out, _ = _run(inputs, N_CORES, 4)
    return out.astype(np.float32)
```
